# Optimizing a Trainium2 kernel written in Bass

```python
import jax, jax.numpy as jnp
from jax import lax
import numpy as np

D_MODEL = 1024
BATCH = 4
SEQ = 4096
DEPTH = 4

CHUNK = 64
N_MEM = 256
N_BRANCH = 4
GROUPS = 4
GROUP_W = D_MODEL // 8
MIX_W = GROUPS * GROUP_W
SHORT_K = 3
POOL_WINDOWS = (2, 4, 8, 16)
SGU_BLOCK = 128
CONF_K = 31
D_FF = 2816
XA_HEADS = 4
XA_HEAD_DIM = D_MODEL // XA_HEADS
EPS = 1e-6
IN_COLS = 3 * MIX_W + MIX_W + 2 * MIX_W + 2 * MIX_W
IN_SPLITS = (MIX_W, 2 * MIX_W, 3 * MIX_W, 4 * MIX_W, 6 * MIX_W)

kernel_name = "hybrid_gated_conv_pool_sgu_conformer_block"


def rmsnorm(x, g):
    xf = x.astype(jnp.float32)
    y = xf * lax.rsqrt(jnp.mean(xf * xf, axis=-1, keepdims=True) + EPS)
    return (y * g.astype(jnp.float32)).astype(x.dtype)


def layernorm(x, g, b):
    xf = x.astype(jnp.float32)
    mu = jnp.mean(xf, axis=-1, keepdims=True)
    var = jnp.mean(jnp.square(xf - mu), axis=-1, keepdims=True)
    y = (xf - mu) * lax.rsqrt(var + EPS)
    return (y * g.astype(jnp.float32) + b.astype(jnp.float32)).astype(x.dtype)


def causal_depthwise_conv(x, w, b):
    k, c = w.shape
    y = lax.conv_general_dilated(x, w[:, None, :].astype(x.dtype), window_strides=(1,),
                                 padding=[(k - 1, 0)],
                                 dimension_numbers=('NWC', 'WIO', 'NWC'),
                                 feature_group_count=c)
    return y + b


def swiglu(h, w1, w3, w2):
    return (jax.nn.silu(h @ w1) * (h @ w3)) @ w2


def short_conv_mixer(xa, ba, ca, w, b):
    return ba * causal_depthwise_conv(ca * xa, w, b)


def pool_mixer(p, w_grp, scale):
    bsz, s, _ = p.shape
    pg = p.reshape(bsz, s, GROUPS, GROUP_W)
    pos = jnp.arange(1, s + 1, dtype=jnp.float32)
    outs = []
    for gi, win in enumerate(POOL_WINDOWS):
        xg = pg[:, :, gi].astype(jnp.float32)
        cs = jnp.cumsum(xg, axis=1)
        lag = jnp.pad(cs, ((0, 0), (win, 0), (0, 0)))[:, :s]
        mean = (cs - lag) / jnp.minimum(pos, float(win))[None, :, None]
        outs.append(mean - xg)
    pooled = jnp.stack(outs, axis=2).astype(p.dtype)
    mixed = jnp.einsum('bsgc,gcd->bsgd', pooled, w_grp)
    return mixed.reshape(bsz, s, MIX_W) * scale


def sgu_mixer(gc, ln_g, ln_b, ws, bs):
    bsz, s, _ = gc.shape
    gc = jax.nn.gelu(gc)
    u, v = jnp.split(gc, 2, axis=-1)
    v = layernorm(v, ln_g, ln_b)
    cidx = jnp.arange(SGU_BLOCK) // CHUNK
    mask = cidx[None, :] <= cidx[:, None]
    wm = jnp.where(mask[None], ws, jnp.zeros_like(ws))
    vb = v.reshape(bsz, s // SGU_BLOCK, SGU_BLOCK, GROUPS, GROUP_W)
    mixed = jnp.einsum('gij,bnjgc->bnigc', wm, vb) + bs.T[None, None, :, :, None]
    return u * mixed.reshape(bsz, s, MIX_W)


def conformer_conv_mixer(d, w, b, ln_g, ln_b):
    a, gate = jnp.split(d, 2, axis=-1)
    y = causal_depthwise_conv(a * jax.nn.sigmoid(gate), w, b)
    return jax.nn.silu(layernorm(y, ln_g, ln_b))


def memory_cross_attention(h, mem_n, wq, wk, wv, wo):
    bsz, s, _ = h.shape
    q = (h @ wq).reshape(bsz, s, XA_HEADS, XA_HEAD_DIM)
    k = (mem_n @ wk).reshape(bsz, -1, XA_HEADS, XA_HEAD_DIM)
    v = (mem_n @ wv).reshape(bsz, -1, XA_HEADS, XA_HEAD_DIM)
    scores = jnp.einsum('bshd,bmhd->bhsm', q.astype(jnp.float32), k.astype(jnp.float32))
    probs = jax.nn.softmax(scores * (XA_HEAD_DIM ** -0.5), axis=-1).astype(v.dtype)
    o = jnp.einsum('bhsm,bmhd->bshd', probs, v).reshape(bsz, s, D_MODEL)
    return o @ wo


def setup_inputs(seed: int = 0) -> dict:
    key = jax.random.key(seed)
    ks = iter(jax.random.split(key, 64))
    L, D = DEPTH, D_MODEL

    def w(shape, fan_in):
        return jax.random.normal(next(ks), shape, jnp.float32) * (fan_in ** -0.5)

    def gain(shape):
        return 1.0 + 0.05 * jax.random.normal(next(ks), shape, jnp.float32)

    def bias(shape, s=0.02):
        return s * jax.random.normal(next(ks), shape, jnp.float32)

    return {
        'x': jax.random.normal(next(ks), (BATCH, SEQ, D), jnp.float32),
        'mem': jax.random.normal(next(ks), (BATCH, N_MEM, D), jnp.float32),
        'ffn1_pre_g': gain((L, D)),
        'ffn1_post_g': gain((L, D)),
        'ffn1_w1': w((L, D, D_FF), D),
        'ffn1_w3': w((L, D, D_FF), D),
        'ffn1_w2': w((L, D_FF, D), D_FF),
        'mix_pre_g': gain((L, D)),
        'mix_post_g': gain((L, D)),
        'w_in': w((L, D, IN_COLS), D),
        'conv_a_w': w((L, SHORT_K, MIX_W), SHORT_K),
        'conv_a_b': bias((L, MIX_W)),
        'pool_w': w((L, GROUPS, GROUP_W, GROUP_W), GROUP_W),
        'pool_scale': gain((L, MIX_W)),
        'sgu_ln_g': gain((L, MIX_W)),
        'sgu_ln_b': bias((L, MIX_W)),
        'sgu_ws': w((L, GROUPS, SGU_BLOCK, SGU_BLOCK), SGU_BLOCK),
        'sgu_b': 1.0 + bias((L, GROUPS, SGU_BLOCK), 0.05),
        'conv_d_w': w((L, CONF_K, MIX_W), CONF_K),
        'conv_d_b': bias((L, MIX_W)),
        'conv_d_ln_g': gain((L, MIX_W)),
        'conv_d_ln_b': bias((L, MIX_W)),
        'w_branch': w((L, N_BRANCH, MIX_W, D), MIX_W),
        'w_gate': w((L, D, N_BRANCH * D), D),
        'b_gate': bias((L, N_BRANCH * D)),
        'w_o': w((L, D, D), D),
        'xa_pre_g': gain((L, D)),
        'xa_post_g': gain((L, D)),
        'mem_g': gain((L, D)),
        'xa_wq': w((L, D, D), D),
        'xa_wk': w((L, D, D), D),
        'xa_wv': w((L, D, D), D),
        'xa_wo': w((L, D, D), D),
        'ffn2_pre_g': gain((L, D)),
        'ffn2_post_g': gain((L, D)),
        'ffn2_w1': w((L, D, D_FF), D),
        'ffn2_w3': w((L, D, D_FF), D),
        'ffn2_w2': w((L, D_FF, D), D_FF),
    }


def reference(x, mem, ffn1_pre_g, ffn1_post_g, ffn1_w1, ffn1_w3, ffn1_w2,
              mix_pre_g, mix_post_g, w_in, conv_a_w, conv_a_b, pool_w, pool_scale,
              sgu_ln_g, sgu_ln_b, sgu_ws, sgu_b, conv_d_w, conv_d_b, conv_d_ln_g, conv_d_ln_b,
              w_branch, w_gate, b_gate, w_o,
              xa_pre_g, xa_post_g, mem_g, xa_wq, xa_wk, xa_wv, xa_wo,
              ffn2_pre_g, ffn2_post_g, ffn2_w1, ffn2_w3, ffn2_w2):
    bsz, s, _ = x.shape
    for l in range(DEPTH):
        h = rmsnorm(x, ffn1_pre_g[l])
        x = x + 0.5 * rmsnorm(swiglu(h, ffn1_w1[l], ffn1_w3[l], ffn1_w2[l]), ffn1_post_g[l])

        h = rmsnorm(x, mix_pre_g[l])
        z = h @ w_in[l]
        xa, ba, ca, p, gc, d = jnp.split(z, IN_SPLITS, axis=-1)
        y_a = short_conv_mixer(xa, ba, ca, conv_a_w[l], conv_a_b[l]) @ w_branch[l, 0]
        y_b = pool_mixer(p, pool_w[l], pool_scale[l]) @ w_branch[l, 1]
        y_c = sgu_mixer(gc, sgu_ln_g[l], sgu_ln_b[l], sgu_ws[l], sgu_b[l]) @ w_branch[l, 2]
        y_d = conformer_conv_mixer(d, conv_d_w[l], conv_d_b[l], conv_d_ln_g[l],
                                   conv_d_ln_b[l]) @ w_branch[l, 3]
        gates = jax.nn.sigmoid(h @ w_gate[l] + b_gate[l]).reshape(bsz, s, N_BRANCH, D_MODEL)
        merged = (gates[:, :, 0] * y_a + gates[:, :, 1] * y_b
                  + gates[:, :, 2] * y_c + gates[:, :, 3] * y_d)
        x = x + rmsnorm(merged @ w_o[l], mix_post_g[l])

        h = rmsnorm(x, xa_pre_g[l])
        mem_n = rmsnorm(mem, mem_g[l])
        xa_out = memory_cross_attention(h, mem_n, xa_wq[l], xa_wk[l], xa_wv[l], xa_wo[l])
        x = x + rmsnorm(xa_out, xa_post_g[l])

        h = rmsnorm(x, ffn2_pre_g[l])
        x = x + 0.5 * rmsnorm(swiglu(h, ffn2_w1[l], ffn2_w3[l], ffn2_w2[l]), ffn2_post_g[l])
    return x
```

```python
import contextlib
import numpy as np
import concourse.bass as bass
import concourse.mybir as mybir
from concourse.bass_utils import run_bass_kernel_spmd

F32 = mybir.dt.float32
BF16 = mybir.dt.bfloat16
AF = mybir.ActivationFunctionType
ALU = mybir.AluOpType

NCORES = 8
DEPTH = 4
D = 1024
SEQ = 4096
HALO = 128
TOK = 2048 + HALO
GROUPS = [(0, 768), (768, 768), (1536, 640)]
NMEM = 256
DFF = 2816
NH = DFF // 128
EPS = 1e-6
SLOT = 4096
NSLOT = 4


def blocks(G):
    return [(0, 384), (384, G - 384)]


COL = {}
_o = 0
for _n, _w in [('ffn1_pre', 8), ('ffn1_post', 8), ('mix_pre', 8), ('mix_post', 8), ('xa_pre', 8),
               ('xa_post', 8), ('mem_g', 8), ('ffn2_pre', 8), ('ffn2_post', 8), ('b_gate', 32),
               ('conv_a_w', 12), ('conv_a_b', 4), ('pool_scale', 4), ('conv_d_w', 124),
               ('conv_d_b', 4), ('conv_d_ln_g', 4), ('conv_d_ln_b', 4)]:
    COL[_n] = _o
    _o += _w
NCOL = _o

ITEMS = []
for _p in ('w', 'v'):
    pass


def _items():
    it = []
    it += [(('f1_13', i), 4096) for i in range(11)]
    it += [(('f1_2', c), 2816) for c in range(8)]
    it += [(('inA', j), 3072) for j in range(4)]
    it += [('inB', 4096), ('poolw', 512), ('inCu', 4096), ('inCv', 4096), ('wsT', 512)]
    it += [(('inD', j), 4096) for j in range(2)]
    for c in range(8):
        it += [(('gate', c), 4096), (('branch', c), 2048)]
    it += [(('wo', h), 4096) for h in range(2)]
    for n in ('wk', 'wv', 'wq', 'xo'):
        it += [((n, h), 4096) for h in range(2)]
    it += [(('f2_13', i), 4096) for i in range(11)]
    it += [(('f2_2', c), 2816) for c in range(8)]
    return it


ITEMS = _items()
ITEM_OFF = {}
_o = 0
for _n, _l in ITEMS:
    ITEM_OFF[_n] = (_o, _l)
    _o += _l
WLEN = _o


class Ctx:
    def __init__(self, nc, stack):
        self.nc = nc
        self.stack = stack
        self.eng = {'pe': nc.tensor, 'dve': nc.vector, 'act': nc.scalar,
                    'pool': nc.gpsimd, 'sp': nc.sync}
        self.sem = {}
        self.cnt = {}
        self.unordered = set()
        for e in self.eng:
            self.sem[e] = stack.enter_context(nc.semaphore("s_" + e))
            self.cnt[e] = 0
        self.known = {e: {} for e in self.eng}
        self.last_w = {}
        self.readers = {}
        self.nwaits = 0
        self.nins = 0
        self.psi = 0

    def new_sem(self, name, unordered=False):
        self.sem[name] = self.stack.enter_context(self.nc.semaphore(name))
        self.cnt[name] = 0
        if unordered:
            self.unordered.add(name)
        return name

    def sbuf(self, name, shape, dt):
        return self.stack.enter_context(self.nc.sbuf_tensor(name, list(shape), dt))

    def psum(self, name, shape, dt):
        return self.stack.enter_context(self.nc.psum_tensor(name, list(shape), dt))

    def _deps(self, reads, writes):
        deps = {}
        for k in reads:
            w = self.last_w.get(k)
            if w and deps.get(w[0], 0) < w[1]:
                deps[w[0]] = w[1]
        for k in writes:
            w = self.last_w.get(k)
            if w and deps.get(w[0], 0) < w[1]:
                deps[w[0]] = w[1]
            r = self.readers.get(k)
            if r:
                for s, v in r.items():
                    if deps.get(s, 0) < v:
                        deps[s] = v
        return deps

    def _emit_waits(self, e, deps):
        eng = self.eng[e]
        kn = self.known[e]
        for s, v in deps.items():
            if s == 'pe' and e == 'pe':
                continue
            if s in self.unordered:
                v = self.cnt[s]
            if kn.get(s, 0) >= v:
                continue
            eng.wait_ge(self.sem[s], v)
            kn[s] = v
            self.nwaits += 1

    def _record(self, s, v, reads, writes):
        for k in writes:
            self.last_w[k] = (s, v)
            self.readers[k] = {}
        for k in reads:
            r = self.readers.setdefault(k, {})
            if r.get(s, 0) < v:
                r[s] = v

    def op(self, e, fn, reads=(), writes=()):
        self._emit_waits(e, self._deps(reads, writes))
        ins = fn(self.eng[e])
        self.cnt[e] += 1
        ins.then_inc(self.sem[e], 1)
        self.nins += 1
        self._record(e, self.cnt[e], reads, writes)

    def dma(self, q, sem, out, in_, reads=(), writes=()):
        self._emit_waits(q, self._deps(reads, writes))
        ins = self.eng[q].dma_start(out=out, in_=in_)
        self.cnt[sem] += 16
        ins.then_inc(self.sem[sem], 16)
        self.nins += 1
        self._record(sem, self.cnt[sem], reads, writes)

    def mm(self, groups, reads, writes):
        self._emit_waits('pe', self._deps(reads, writes))
        ins = None
        for out, pairs in groups:
            n = len(pairs)
            for i, (l, r) in enumerate(pairs):
                ins = self.nc.tensor.matmul(out, l, r, start=(i == 0), stop=(i == n - 1))
                self.nins += 1
        self.cnt['pe'] += 1
        ins.then_inc(self.sem['pe'], 1)
        self._record('pe', self.cnt['pe'], reads, writes)

    def barrier(self):
        for e in ('dve', 'act'):
            self._emit_waits(e, {f: self.cnt[f] for f in ('pe', 'dve', 'act') if self.cnt[f] > 0})

    def wait_all(self, e, sems):
        self._emit_waits(e, {s: self.cnt[s] for s in sems if self.cnt[s] > 0})


class Arena:
    def __init__(self, t, words):
        self.t = t
        self.words = words
        self.off = 0

    def f32(self, n):
        assert self.off + n <= self.words, ("arena overflow", self.off, n, self.words)
        ap = self.t[:, self.off:self.off + n]
        self.off += n
        return ap

    def bf16(self, n):
        nw = (n + 1) // 2
        assert self.off + nw <= self.words, ("arena overflow", self.off, nw, self.words)
        ap = self.t[:, self.off:self.off + nw].bitcast(BF16)
        self.off += nw
        return ap[:, 0:n]


def build_program(nlayers=DEPTH, stop_after=None):
    nc = bass.Bass("TRN2", target_bir_lowering=False)
    x_d = nc.dram_tensor("xT", [128, 8, TOK], F32, kind="ExternalInput").ap()
    mem_d = nc.dram_tensor("memT", [128, 8 * NMEM], F32, kind="ExternalInput").ap()
    ws_d = nc.dram_tensor("wstream", [DEPTH, 128, WLEN], F32, kind="ExternalInput").ap()
    cols_d = nc.dram_tensor("cols", [128, DEPTH * NCOL], F32, kind="ExternalInput").ap()
    bc_d = nc.dram_tensor("bc", [DEPTH, 128, 1536], F32, kind="ExternalInput").ap()
    misc_d = nc.dram_tensor("miscc", [128, 65 + 128], F32, kind="ExternalInput").ap()
    y_d = nc.dram_tensor("yT", [128, 8, TOK], F32, kind="ExternalOutput").ap()

    with contextlib.ExitStack() as st:
        k = Ctx(nc, st)
        xT = k.sbuf("xTs", [128, 8, TOK], F32)
        hT = k.sbuf("hTs", [128, 8, 768], BF16)
        slots = [k.sbuf(f"slot{i}", [128, SLOT], BF16) for i in range(NSLOT)]
        ssem = [k.new_sem(f"ws{i}") for i in range(NSLOT)]
        k.new_sem("misc", unordered=True)
        k.new_sem("outs", unordered=True)
        cols = k.sbuf("cols_s", [128, DEPTH * NCOL], F32)
        bct = k.sbuf("bc_s", [128, 1536], F32)
        miscc = k.sbuf("misc_s", [128, 65 + 128], F32)
        onesD = k.sbuf("onesD", [128, 128], BF16)
        ones512 = k.sbuf("ones512", [128, 128], BF16)
        ones1 = k.sbuf("ones1", [128, 128], BF16)
        carA = k.sbuf("carA", [128, 4, 2], F32)
        carB = k.sbuf("carB", [128, 4, 15], F32)
        carD = k.sbuf("carD", [128, 4, 30], BF16)
        kT = k.sbuf("kTs", [128, 8, NMEM], BF16)
        vtok = k.sbuf("vtok", [128, 2, D], BF16)
        small = k.sbuf("small", [128, 32], F32)
        AW = (nc.sbuf_bytes_remaining - 1024) // 4
        arena_t = k.sbuf("arena", [128, AW], F32)
        ar = Arena(arena_t, AW)
        ps = [k.psum(f"ps{i}", [128, 512], F32) for i in range(8)]

        pend = {'q': [], 'busy': False}

        def run_one():
            due, fn = pend['q'].pop(0)
            pend['busy'] = True
            fn()
            pend['busy'] = False

        def defer(fn, delay):
            pend['q'].append((k.psi + delay, fn))

        def flush_all():
            assert not pend['busy']
            while pend['q']:
                run_one()

        def barrier():
            flush_all()
            k.barrier()

        def PS():
            if not pend['busy']:
                while pend['q'] and pend['q'][0][0] <= k.psi:
                    run_one()
            i = k.psi % 8
            k.psi += 1
            return ps[i], ('ps', i)

        def col(l, name, idx):
            o = l * NCOL + COL[name] + idx
            return cols[:, o:o + 1]

        def act(out, in_, func, reads, writes, **kw):
            k.op('act', lambda e: e.activation(out=out, in_=in_, func=func, **kw), reads, writes)

        def tt(out, in0, in1, op, reads, writes):
            k.op('dve', lambda e: e.tensor_tensor(out=out, in0=in0, in1=in1, op=op), reads, writes)

        def ts(out, in0, s1, s2, op0, op1, reads, writes):
            if s2 is None:
                k.op('dve', lambda e: e.tensor_scalar(out=out, in0=in0, scalar1=s1, scalar2=None, op0=op0),
                     reads, writes)
            else:
                k.op('dve', lambda e: e.tensor_scalar(out=out, in0=in0, scalar1=s1, scalar2=s2, op0=op0, op1=op1),
                     reads, writes)

        def stt(out, in0, scalar, in1, op0, op1, reads, writes):
            k.op('dve', lambda e: e.scalar_tensor_tensor(out=out, in0=in0, scalar=scalar, in1=in1, op0=op0, op1=op1),
                 reads, writes)

        def recip(out, in_, reads, writes):
            k.op('dve', lambda e: e.reciprocal(out=out, in_=in_), reads, writes)

        def vcopy(out, in_, reads, writes):
            k.op('dve', lambda e: e.tensor_copy(out=out, in_=in_), reads, writes)

        def vmemset(ap, val, writes):
            k.op('dve', lambda e: e.memset(ap, val), (), writes)

        wstate = {'n': 0}

        def witem(l, name):
            off, ln = ITEM_OFF[name]
            s = wstate['n'] % NSLOT
            wstate['n'] += 1
            k.dma('pool', ssem[s], slots[s][:, 0:ln], ws_d[l, :, off:off + ln], writes=[('slot', s)])
            return slots[s][:, 0:ln], ('slot', s)

        def units(ap, nu, kc):
            return ap.rearrange("p (u k n) -> p u k n", u=nu, k=kc, n=128)

        k.dma('sp', 'misc', cols[:, :], cols_d[:, :], writes=['cols'])
        k.dma('sp', 'misc', miscc[:, :], misc_d[:, :], writes=['miscc'])
        xsem = [k.new_sem(f"xl{B}") for B in range(6)]
        for gi_, (g0_, G_) in enumerate(GROUPS):
            for bi_, (b0_, bn_) in enumerate(blocks(G_)):
                B_ = gi_ * 2 + bi_
                t0_ = g0_ + b0_
                k.dma('sp', xsem[B_], xT[:, :, t0_:t0_ + bn_], x_d[:, :, t0_:t0_ + bn_],
                      writes=[('x', c, B_) for c in range(8)])
        vmemset(onesD[:], 1.0 / D, ['ones'])
        vmemset(ones512[:], 1.0 / 512, ['ones'])
        vmemset(ones1[:], 1.0, ['ones'])
        hmask = miscc[:, 0:1]
        pinv = miscc[:, 1:65].rearrange("p (g t) -> p g t", g=4)
        ident = miscc[:, 65:193]

        def arena_reset():
            barrier()
            ar.off = 0

        def squares(srcs, n, sq, rkeys):
            for c, s_ in enumerate(srcs):
                act(sq[:, c, 0:n], s_, AF.Square, [rkeys[c]], [('sq', c)])

        def rstd_from_sq(n, sq, rs, tag):
            p, pk = PS()
            k.mm([(p[:, 0:n], [(onesD[:], sq[:, c, 0:n]) for c in range(8)])],
                 ['ones'] + [('sq', c) for c in range(8)], [pk])
            act(rs[:, 0:n], p[:, 0:n], AF.Ln, [pk], [('rs', tag)], bias=EPS, scale=1.0)
            act(rs[:, 0:n], rs[:, 0:n], AF.Exp, [('rs', tag)], [('rs', tag)], scale=-0.5)

        def stats_rstd(srcs, n, sq, rs, rkeys, tag):
            squares(srcs, n, sq, rkeys)
            rstd_from_sq(n, sq, rs, tag)

        def norm_chain(BLK, srcs_fn, rkeys_fn, apply_fn, sq, rs, lazy):
            flush_all()
            squares(srcs_fn(0), BLK[0][1], sq, rkeys_fn(0))

            def stage(bi):
                def fn():
                    rstd_from_sq(BLK[bi][1], sq, rs, 'r')
                    apply_fn(bi)
                    if bi + 1 < len(BLK):
                        squares(srcs_fn(bi + 1), BLK[bi + 1][1], sq, rkeys_fn(bi + 1))
                        defer(stage(bi + 1), 4)
                return fn
            defer(stage(0), 4)
            if not lazy:
                flush_all()

        def prenorm(l, gi, gname, sq, rs, lazy=False):
            g0, G = GROUPS[gi]
            BLK = blocks(G)

            def apply(bi):
                b0, bn = BLK[bi]
                B = gi * 2 + bi
                t0 = g0 + b0
                for c in range(8):
                    stt(hT[:, c, b0:b0 + bn], xT[:, c, t0:t0 + bn], col(l, gname, c), rs[:, 0:bn],
                        ALU.mult, ALU.mult, [('x', c, B), ('rs', 'r'), 'cols'], [('h', c, bi)])
            norm_chain(BLK, lambda bi: [xT[:, c, g0 + BLK[bi][0]:g0 + BLK[bi][0] + BLK[bi][1]] for c in range(8)],
                       lambda bi: [('x', c, gi * 2 + bi) for c in range(8)], apply, sq, rs, lazy)

        def epilogue(l, gi, gname, coef, outT, sq, rs, lazy=False):
            g0, G = GROUPS[gi]
            BLK = blocks(G)

            def apply(bi):
                b0, bn = BLK[bi]
                B = gi * 2 + bi
                t0 = g0 + b0
                for c in range(8):
                    stt(outT[:, c, b0:b0 + bn], outT[:, c, b0:b0 + bn], col(l, gname, c), rs[:, 0:bn],
                        ALU.mult, ALU.mult, [('o', c, bi), ('rs', 'r'), 'cols'], [('o', c, bi)])
                    stt(xT[:, c, t0:t0 + bn], outT[:, c, b0:b0 + bn], coef, xT[:, c, t0:t0 + bn],
                        ALU.mult, ALU.add, [('o', c, bi), ('x', c, B)], [('x', c, B)])
            norm_chain(BLK, lambda bi: [outT[:, c, BLK[bi][0]:BLK[bi][0] + BLK[bi][1]] for c in range(8)],
                       lambda bi: [('o', c, bi) for c in range(8)], apply, sq, rs, lazy)

        def proj_out(l, gi, items, nk, src, src_keys, outT):
            g0, G = GROUPS[gi]
            cp = 0
            for name, nu in items:
                w, wk = witem(l, name)
                wu = units(w, nu, nk)
                for u in range(nu):
                    for bi, (b0, bn) in enumerate(blocks(G)):
                        p, pk = PS()
                        k.mm([(p[:, 0:bn], [(wu[:, u, kc, :], src[:, kc, b0:b0 + bn]) for kc in range(nk)])],
                             [wk] + src_keys(bi), [pk])
                        act(outT[:, cp, b0:b0 + bn], p[:, 0:bn], AF.Copy, [pk], [('o', cp, bi)])
                    cp += 1

        hk = lambda bi: [('h', c, bi) for c in range(8)]
        GM = 768

        def ffn_phase(l, pfx, pre, post, pre_done, nxt):
            arena_reset()
            gT = ar.bf16(NH * GM).rearrange("p (c t) -> p c t", c=NH)
            outT = ar.f32(8 * GM).rearrange("p (c t) -> p c t", c=8)
            sq = ar.bf16(8 * 384).rearrange("p (c t) -> p c t", c=8)
            rs = ar.f32(384)
            sa = [ar.f32(384), ar.f32(384)]
            cnt = {'n': 0}

            def s1(gi):
                g0, G = GROUPS[gi]
                for it in range(11):
                    if it == 1:
                        flush_all()
                    w, wk = witem(l, (pfx + '_13', it))
                    wu = units(w, 4, 8)
                    for u in range(2):
                        j = 2 * it + u
                        for bi, (b0, bn) in enumerate(blocks(G)):
                            pa, pak = PS()
                            k.mm([(pa[:, 0:bn], [(wu[:, u, kc, :], hT[:, kc, b0:b0 + bn]) for kc in range(8)])],
                                 [wk] + hk(bi), [pak])
                            s = cnt['n'] % 2
                            cnt['n'] += 1
                            act(sa[s][:, 0:bn], pa[:, 0:bn], AF.Silu, [pak], [('sa', s)])
                            pb, pbk = PS()
                            k.mm([(pb[:, 0:bn], [(wu[:, 2 + u, kc, :], hT[:, kc, b0:b0 + bn]) for kc in range(8)])],
                                 [wk] + hk(bi), [pbk])
                            tt(gT[:, j, b0:b0 + bn], pb[:, 0:bn], sa[s][:, 0:bn], ALU.mult,
                               [pbk, ('sa', s)], [('g', j, bi)])

            if not pre_done:
                prenorm(l, 0, pre, sq, rs)
            for gi in range(3):
                s1(gi)
                if gi < 2:
                    prenorm(l, gi + 1, pre, sq, rs, lazy=True)
                elif nxt is not None:
                    prenorm(nxt[0], 0, nxt[1], sq, rs, lazy=True)
                proj_out(l, gi, [((pfx + '_2', c), 1) for c in range(8)], NH, gT,
                         lambda bi: [('g', j, bi) for j in range(NH)], outT)
                epilogue(l, gi, post, 0.5, outT, sq, rs, lazy=(gi < 2))

        def mixer_phase(l, pre_done, nxt):
            arena_reset()
            k.dma('sp', k.new_sem(f"bc{l}"), bct[:, :], bc_d[l, :, :], writes=['bc'])
            sbr = [ar.bf16(4 * GM).rearrange("p (c t) -> p c t", c=4) for _ in range(4)]
            merged = ar.bf16(8 * GM).rearrange("p (c t) -> p c t", c=8)
            sq = ar.bf16(8 * 384).rearrange("p (c t) -> p c t", c=8)
            rs = ar.f32(384)
            mark = ar.off
            if not pre_done:
                prenorm(l, 0, 'mix_pre', sq, rs)
            for gi in range(3):
                mixer_group(l, gi, sbr, merged, sq, rs, mark, nxt)

        def mixer_group(l, gi, sbr, merged, sq, rs, mark, nxt):
            g0, G = GROUPS[gi]
            BL = blocks(G)

            def zmm(wu, u, bi, wk):
                b0, bn = BL[bi]
                p, pk = PS()
                k.mm([(p[:, 0:bn], [(wu[:, u, kc, :], hT[:, kc, b0:b0 + bn]) for kc in range(8)])],
                     [wk] + hk(bi), [pk])
                return p, pk

            def carry(buf, H, car, j, key, ckey):
                if gi == 0:
                    vmemset(buf[:, 0:H], 0.0, [key])
                    ts(buf[:, H:H + HALO], buf[:, H:H + HALO], hmask, None, ALU.mult, None,
                       [key, 'miscc'], [key])
                else:
                    vcopy(buf[:, 0:H], car[:, j, :], [('car', ckey, j)], [key])
                if gi < 2:
                    vcopy(car[:, j, :], buf[:, G:G + H], [key], [('car', ckey, j)])

            ar.off = mark + 0
            barrier()
            xa_sb = [ar.f32(GM), ar.f32(GM)]
            cxs = [ar.f32(2 + GM), ar.f32(2 + GM)]
            accs = [ar.f32(GM), ar.f32(GM)]
            sA = sbr[0]
            for j in range(4):
                d = j % 2
                xs, cx, acc = xa_sb[d], cxs[d], accs[d]
                kx_, kc_, ka_ = ('xa_sb', d), ('cx', d), ('accA', d)
                w, wk = witem(l, ('inA', j))
                wu = units(w, 3, 8)
                for bi, (b0, bn) in enumerate(BL):
                    p, pk = zmm(wu, 0, bi, wk)
                    act(xs[:, b0:b0 + bn], p[:, 0:bn], AF.Copy, [pk], [kx_])
                    p2, pk2 = zmm(wu, 2, bi, wk)
                    tt(cx[:, 2 + b0:2 + b0 + bn], p2[:, 0:bn], xs[:, b0:b0 + bn], ALU.mult,
                       [pk2, kx_], [kc_])
                carry(cx, 2, carA, j, kc_, 'A')
                act(acc[:, 0:G], cx[:, 2:2 + G], AF.Identity, [kc_, 'cols'], [ka_],
                    scale=col(l, 'conv_a_w', 2 * 4 + j), bias=col(l, 'conv_a_b', j))
                stt(acc[:, 0:G], cx[:, 1:1 + G], col(l, 'conv_a_w', 1 * 4 + j), acc[:, 0:G], ALU.mult, ALU.add,
                    [kc_, ka_], [ka_])
                stt(acc[:, 0:G], cx[:, 0:G], col(l, 'conv_a_w', 0 * 4 + j), acc[:, 0:G], ALU.mult, ALU.add,
                    [kc_, ka_], [ka_])
                for bi, (b0, bn) in enumerate(BL):
                    p, pk = zmm(wu, 1, bi, wk)
                    tt(sA[:, j, b0:b0 + bn], p[:, 0:bn], acc[:, b0:b0 + bn], ALU.mult, [pk, ka_],
                       [('s', 0, j, bi)])

            ar.off = mark + 0
            barrier()
            pbufs = [ar.f32(15 + GM), ar.f32(15 + GM)]
            tas = [ar.f32(15 + GM), ar.f32(15 + GM)]
            tbs = [ar.f32(15 + GM), ar.f32(15 + GM)]
            pooleds = [ar.bf16(GM), ar.bf16(GM)]
            t16 = ar.f32(32)
            sB = sbr[1]
            w, wk = witem(l, 'inB')
            wu = units(w, 4, 8)
            pw, pwk = witem(l, 'poolw')
            pwu = pw.rearrange("p (g d) -> p g d", g=4)
            L = 15 + G
            for pair in ((0, 1), (2, 3)):
                for g in pair:
                    d = g % 2
                    for bi, (b0, bn) in enumerate(BL):
                        p, pk = zmm(wu, g, bi, wk)
                        act(pbufs[d][:, 15 + b0:15 + b0 + bn], p[:, 0:bn], AF.Copy, [pk], [('pbuf', d)])
                for g in pair:
                    d = g % 2
                    carry(pbufs[d], 15, carB, g, ('pbuf', d), 'B')
                cur = {}
                for g in pair:
                    d = g % 2
                    tt(tas[d][:, 1:L], pbufs[d][:, 1:L], pbufs[d][:, 0:L - 1], ALU.add, [('pbuf', d)], [('ta', d)])
                    cur[g] = (tas[d], ('ta', d), tbs[d], ('tb', d), 1)
                for step in range(3):
                    for g in pair:
                        if step < g:
                            c_, ck_, o_, ok_, sh = cur[g]
                            sh2 = sh * 2
                            lo = 2 * sh2 - 1
                            tt(o_[:, lo:L], c_[:, lo:L], c_[:, lo - sh2:L - sh2], ALU.add, [ck_], [ok_])
                            cur[g] = (o_, ok_, c_, ck_, sh2)
                for g in pair:
                    d = g % 2
                    win = 2 << g
                    c_, ck_ = cur[g][0], cur[g][1]
                    stt(pooleds[d][:, 0:G], c_[:, 15:L], 1.0 / win, pbufs[d][:, 15:L], ALU.mult, ALU.subtract,
                        [ck_, ('pbuf', d)], [('pooled', d)])
                    if gi == 0:
                        tt(t16[:, d * 16:d * 16 + 16], c_[:, 15 + HALO:15 + HALO + 16], pinv[:, g, :], ALU.mult,
                           [ck_, 'miscc'], [('t16', d)])
                        tt(pooleds[d][:, HALO:HALO + 16], t16[:, d * 16:d * 16 + 16],
                           pbufs[d][:, 15 + HALO:15 + HALO + 16], ALU.subtract,
                           [('t16', d), ('pbuf', d), ('pooled', d)], [('pooled', d)])
                for g in pair:
                    d = g % 2
                    for bi, (b0, bn) in enumerate(BL):
                        p, pk = PS()
                        k.mm([(p[:, 0:bn], [(pwu[:, g, :], pooleds[d][:, b0:b0 + bn])])], [pwk, ('pooled', d)], [pk])
                        act(sB[:, g, b0:b0 + bn], p[:, 0:bn], AF.Copy, [pk, 'cols'], [('s', 1, g, bi)],
                            scale=col(l, 'pool_scale', g))

            ar.off = mark + 0
            barrier()
            uT = ar.bf16(4 * GM).rearrange("p (c t) -> p c t", c=4)
            vts = [ar.f32(512) for _ in range(3)]
            vns = [ar.bf16(512) for _ in range(3)]
            gscs = [ar.f32(512) for _ in range(3)]
            sC = sbr[2]

            def gelu_tanh(out, p, sc, pk, wkeys, sk):
                act(sc, p, AF.Square, [pk], [sk], scale=0.21145921592590706)
                stt(sc, sc, 1.0, p, ALU.add, ALU.mult, [sk, pk], [sk])
                act(sc, sc, AF.Sigmoid, [sk], [sk], scale=1.5957691216057308)
                tt(out, sc, p, ALU.mult, [sk, pk], wkeys)

            w, wk = witem(l, 'inCu')
            wu = units(w, 4, 8)
            nu_ = 0
            for j in range(4):
                for bi, (b0, bn) in enumerate(BL):
                    p, pk = zmm(wu, j, bi, wk)
                    d = nu_ % 2
                    nu_ += 1
                    gelu_tanh(uT[:, j, b0:b0 + bn], p[:, 0:bn], gscs[d][:, 0:bn], pk, [('u', j, bi)], ('gsc', d))
            wv_, wvk = witem(l, 'inCv')
            wvu = wv_.rearrange("p (k n) -> p k n", k=8)
            wst, wstk = witem(l, 'wsT')
            wsu = wst.rearrange("p (g i) -> p g i", g=4)
            vmemset(wsu[64:128, :, 0:64], 0.0, [wstk])
            lng = bct[:, 0:512]
            lnb = bct[:, 512:1024]
            sgb = bct[:, 1024:1536]
            ntile = G // 128
            for t0_ in range(0, ntile, 3):
                tiles = list(range(t0_, min(t0_ + 3, ntile)))
                pp = {}
                for i, tti in enumerate(tiles):
                    tk0 = tti * 128
                    bi = 0 if tk0 < 384 else 1
                    p, pk = PS()
                    k.mm([(p[:, :], [(hT[:, kc, tk0:tk0 + 128], wvu[:, kc, :]) for kc in range(8)])],
                         [wvk] + hk(bi), [pk])
                    pp[i] = (p, pk)
                for i in pp:
                    act(gscs[i][:, :], pp[i][0][:, :], AF.Square, [pp[i][1]], [('gsc', i)], scale=0.21145921592590706)
                for i in pp:
                    stt(gscs[i][:, :], gscs[i][:, :], 1.0, pp[i][0][:, :], ALU.add, ALU.mult,
                        [('gsc', i), pp[i][1]], [('gsc', i)])
                for i in pp:
                    act(gscs[i][:, :], gscs[i][:, :], AF.Sigmoid, [('gsc', i)], [('gsc', i)], scale=1.5957691216057308)
                for i in pp:
                    tt(vts[i][:, :], gscs[i][:, :], pp[i][0][:, :], ALU.mult, [('gsc', i), pp[i][1]], [('vt', i)])
                for i in pp:
                    o = 10 * i
                    k.op('dve', lambda e: e.bn_stats(out=small[:, o:o + 6], in_=vts[i][:, :]), [('vt', i)], [('bst', i)])
                for i in pp:
                    o = 10 * i
                    k.op('dve', lambda e: e.bn_aggr(out=small[:, o + 6:o + 8], in_=small[:, o:o + 6]),
                         [('bst', i)], [('bag', i)])
                for i in pp:
                    o = 10 * i
                    act(small[:, o + 8:o + 9], small[:, o + 7:o + 8], AF.Sqrt, [('bag', i)], [('sd', i)],
                        bias=EPS, scale=1.0)
                for i in pp:
                    o = 10 * i
                    recip(small[:, o + 8:o + 9], small[:, o + 8:o + 9], [('sd', i)], [('sd', i)])
                for i in pp:
                    o = 10 * i
                    stt(small[:, o + 9:o + 10], small[:, o + 6:o + 7], -1.0, small[:, o + 8:o + 9],
                        ALU.mult, ALU.mult, [('bag', i), ('sd', i)], [('nmr', i)])
                for i in pp:
                    o = 10 * i
                    act(vts[i][:, :], vts[i][:, :], AF.Identity, [('vt', i), ('sd', i), ('nmr', i)], [('vt', i)],
                        scale=small[:, o + 8:o + 9], bias=small[:, o + 9:o + 10])
                for i in pp:
                    tt(vts[i][:, :], vts[i][:, :], lng, ALU.mult, [('vt', i), 'bc'], [('vt', i)])
                for i in pp:
                    tt(vns[i][:, :], vts[i][:, :], lnb, ALU.add, [('vt', i), 'bc'], [('vn', i)])
                p2s = {}
                for i in pp:
                    p2, pk2 = PS()
                    k.mm([(p2[:, g * 128:(g + 1) * 128], [(vns[i][:, g * 128:(g + 1) * 128], wsu[:, g, :])])
                          for g in range(4)], [('vn', i), wstk], [pk2])
                    p2s[i] = (p2, pk2)
                for i in pp:
                    tt(gscs[i][:, :], p2s[i][0][:, :], sgb, ALU.add, [p2s[i][1], 'bc'], [('gsc', i)])
                for i, tti in enumerate(tiles):
                    tk0 = tti * 128
                    bi = 0 if tk0 < 384 else 1
                    tt(sC[:, :, tk0:tk0 + 128], gscs[i][:, :].rearrange("p (g i) -> p g i", g=4),
                       uT[:, :, tk0:tk0 + 128], ALU.mult,
                       [('gsc', i)] + [('u', j, bi) for j in range(4)], [('s', 2, j, bi) for j in range(4)])

            ar.off = mark + 0
            barrier()
            yD = ar.f32(4 * GM).rearrange("p (c t) -> p c t", c=4)
            markD = ar.off
            sig = [ar.f32(384), ar.f32(384)]
            glus = [ar.bf16(32 + GM), ar.bf16(32 + GM)]
            dg = ar.bf16(31 * 128).rearrange("p (k n) -> p k n", k=31)
            sD = sbr[3]
            nsg = 0
            identb = ident[:, :].unsqueeze(1).broadcast_to([128, 31, 128])
            for jj in range(2):
                w, wk = witem(l, ('inD', jj))
                wu = units(w, 4, 8)
                for u in range(2):
                    j = 2 * jj + u
                    d = j % 2
                    glu = glus[d]
                    kg_ = ('glu', d)
                    o = l * NCOL + COL['conv_d_w'] + j
                    wtap = cols[:, o:o + 124:4].unsqueeze(2).broadcast_to([128, 31, 128])
                    tt(dg[:, :, :], identb, wtap, ALU.mult, ['miscc', 'cols'], ['dg'])
                    for bi, (b0, bn) in enumerate(BL):
                        p, pk = zmm(wu, 2 * u + 1, bi, wk)
                        s = nsg % 2
                        nsg += 1
                        act(sig[s][:, 0:bn], p[:, 0:bn], AF.Sigmoid, [pk], [('sig', s)])
                        p2, pk2 = zmm(wu, 2 * u, bi, wk)
                        tt(glu[:, 30 + b0:30 + b0 + bn], p2[:, 0:bn], sig[s][:, 0:bn], ALU.mult,
                           [pk2, ('sig', s)], [kg_])
                    carry(glu, 30, carD, j, kg_, 'D')
                    for bi, (b0, bn) in enumerate(BL):
                        p, pk = PS()
                        k.mm([(p[:, 0:bn], [(dg[:, kk, :], glu[:, kk + b0:kk + b0 + bn]) for kk in range(31)])],
                             ['dg', kg_], [pk])
                        act(yD[:, j, b0:b0 + bn], p[:, 0:bn], AF.Identity, [pk, 'cols'], [('yD', j, bi)],
                            bias=col(l, 'conv_d_b', j), scale=1.0)
            barrier()
            ar.off = markD
            ybf = ar.bf16(4 * 384).rearrange("p (c t) -> p c t", c=4)
            ysq = ar.bf16(4 * 384).rearrange("p (c t) -> p c t", c=4)
            mean_sb = ar.f32(384)
            var_sb = ar.f32(384)
            t1s = [ar.f32(384), ar.f32(384)]
            for bi, (b0, bn) in enumerate(BL):
                for j in range(4):
                    act(ybf[:, j, 0:bn], yD[:, j, b0:b0 + bn], AF.Copy, [('yD', j, bi)], [('ybf', j)])
                    act(ysq[:, j, 0:bn], yD[:, j, b0:b0 + bn], AF.Square, [('yD', j, bi)], [('ysq', j)])
                pm, pmk = PS()
                k.mm([(pm[:, 0:bn], [(ones512[:], ybf[:, j, 0:bn]) for j in range(4)])],
                     ['ones'] + [('ybf', j) for j in range(4)], [pmk])
                pq, pqk = PS()
                k.mm([(pq[:, 0:bn], [(ones512[:], ysq[:, j, 0:bn]) for j in range(4)])],
                     ['ones'] + [('ysq', j) for j in range(4)], [pqk])
                act(mean_sb[:, 0:bn], pm[:, 0:bn], AF.Copy, [pmk], ['mean_sb'])
                tt(var_sb[:, 0:bn], mean_sb[:, 0:bn], mean_sb[:, 0:bn], ALU.mult, ['mean_sb'], ['var_sb'])
                tt(var_sb[:, 0:bn], pq[:, 0:bn], var_sb[:, 0:bn], ALU.subtract, [pqk, 'var_sb'], ['var_sb'])
                ts(var_sb[:, 0:bn], var_sb[:, 0:bn], 0.0, None, ALU.max, None, ['var_sb'], ['var_sb'])
                act(var_sb[:, 0:bn], var_sb[:, 0:bn], AF.Ln, ['var_sb'], ['var_sb'], bias=EPS, scale=1.0)
                act(var_sb[:, 0:bn], var_sb[:, 0:bn], AF.Exp, ['var_sb'], ['var_sb'], scale=-0.5)
                for j in range(4):
                    t1 = t1s[j % 2]
                    tk_ = ('t1', j % 2)
                    tt(t1[:, 0:bn], yD[:, j, b0:b0 + bn], mean_sb[:, 0:bn], ALU.subtract,
                       [('yD', j, bi), 'mean_sb'], [tk_])
                    tt(t1[:, 0:bn], t1[:, 0:bn], var_sb[:, 0:bn], ALU.mult, [tk_, 'var_sb'], [tk_])
                    act(sD[:, j, b0:b0 + bn], t1[:, 0:bn], AF.Silu, [tk_, 'cols'], [('s', 3, j, bi)],
                        scale=col(l, 'conv_d_ln_g', j), bias=col(l, 'conv_d_ln_b', j))

            ar.off = mark + 0
            barrier()
            outT = ar.f32(8 * GM).rearrange("p (c t) -> p c t", c=8)
            gsb = [ar.f32(384), ar.f32(384)]
            macc = ar.f32(384)
            prod1 = ar.f32(384)
            prod = [prod1, prod1]
            ng = 0
            for c in range(8):
                gw, gwk = witem(l, ('gate', c))
                gwu = units(gw, 4, 8)
                bw, bwk = witem(l, ('branch', c))
                bwu = units(bw, 4, 4)
                for bi, (b0, bn) in enumerate(BL):
                    for b in range(4):
                        p, pk = zmm(gwu, b, bi, gwk)
                        s = ng % 2
                        ng += 1
                        act(gsb[s][:, 0:bn], p[:, 0:bn], AF.Sigmoid, [pk, 'cols'], [('gsb', s)],
                            bias=col(l, 'b_gate', b * 8 + c), scale=1.0)
                        p2, pk2 = PS()
                        k.mm([(p2[:, 0:bn], [(bwu[:, b, kc, :], sbr[b][:, kc, b0:b0 + bn]) for kc in range(4)])],
                             [bwk] + [('s', b, kc, bi) for kc in range(4)], [pk2])
                        if b == 0:
                            tt(macc[:, 0:bn], p2[:, 0:bn], gsb[s][:, 0:bn], ALU.mult, [pk2, ('gsb', s)], ['macc'])
                        else:
                            pr, prk = prod[b % 2], ('prod', 0)
                            tt(pr[:, 0:bn], p2[:, 0:bn], gsb[s][:, 0:bn], ALU.mult, [pk2, ('gsb', s)], [prk])
                            if b < 3:
                                tt(macc[:, 0:bn], macc[:, 0:bn], pr[:, 0:bn], ALU.add, ['macc', prk], ['macc'])
                            else:
                                tt(merged[:, c, b0:b0 + bn], macc[:, 0:bn], pr[:, 0:bn], ALU.add,
                                   ['macc', prk], [('m', c, bi)])
            if gi < 2:
                prenorm(l, gi + 1, 'mix_pre', sq, rs, lazy=True)
            elif nxt is not None:
                prenorm(nxt[0], 0, nxt[1], sq, rs, lazy=True)
            proj_out(l, gi, [(('wo', h), 4) for h in range(2)], 8, merged,
                     lambda bi: [('m', c, bi) for c in range(8)], outT)
            epilogue(l, gi, 'mix_post', 1.0, outT, sq, rs)

        def xattn_phase(l, pre_done, nxt):
            arena_reset()
            qT = ar.bf16(8 * GM).rearrange("p (c t) -> p c t", c=8)
            oT = ar.bf16(8 * GM).rearrange("p (c t) -> p c t", c=8)
            kv0 = ar.off
            memt_flat = ar.f32(8 * NMEM)
            memt = memt_flat.rearrange("p (c m) -> p c m", c=8)
            msq = ar.bf16(8 * NMEM).rearrange("p (c m) -> p c m", c=8)
            memn = ar.bf16(8 * NMEM).rearrange("p (c m) -> p c m", c=8)
            mrs = ar.f32(NMEM)
            ar.off = kv0
            outT = ar.f32(8 * GM).rearrange("p (c t) -> p c t", c=8)
            sq = ar.bf16(8 * 384).rearrange("p (c t) -> p c t", c=8)
            rs = ar.f32(384)
            pT = [ar.bf16(2 * 384).rearrange("p (c t) -> p c t", c=2) for _ in range(4)]
            rden = [ar.f32(384) for _ in range(4)]
            k.wait_all('sp', ['pe', 'dve', 'act'])
            k.dma('sp', k.new_sem(f"mem{l}"), memt_flat, mem_d[:, :], writes=['memt'])
            squares([memt[:, c, :] for c in range(8)], NMEM, msq, ['memt'] * 8)
            mk = [('memn', c) for c in range(8)]

            def kv2():
                for h in range(2):
                    w, wk = witem(l, ('wk', h))
                    wu = units(w, 4, 8)
                    for u in range(4):
                        p, pk = PS()
                        k.mm([(p[:, 0:NMEM], [(wu[:, u, kc, :], memn[:, kc, :]) for kc in range(8)])],
                             [wk] + mk, [pk])
                        act(kT[:, 4 * h + u, :], p[:, 0:NMEM], AF.Copy, [pk], ['kT'])
                for h in range(2):
                    w, wk = witem(l, ('wv', h))
                    wu = w.rearrange("p (k n) -> p k n", k=8)
                    for mc in range(2):
                        p, pk = PS()
                        k.mm([(p[:, :], [(memn[:, kc, mc * 128:(mc + 1) * 128], wu[:, kc, :]) for kc in range(8)])],
                             [wk] + mk, [pk])
                        act(vtok[:, mc, h * 512:(h + 1) * 512], p[:, :], AF.Copy, [pk], ['vtok'])

            def kv1():
                rstd_from_sq(NMEM, msq, mrs, 'mem')
                for c in range(8):
                    stt(memn[:, c, :], memt[:, c, :], col(l, 'mem_g', c), mrs[:, :], ALU.mult, ALU.mult,
                        ['memt', ('rs', 'mem'), 'cols'], [('memn', c)])
                defer(kv2, 4)
            defer(kv1, 4)
            na = {'n': 0}

            def qproj(gi):
                g0, G = GROUPS[gi]
                for h in range(2):
                    if h == 1:
                        flush_all()
                    w, wk = witem(l, ('wq', h))
                    wu = units(w, 4, 8)
                    for u in range(4):
                        for bi, (b0, bn) in enumerate(blocks(G)):
                            p, pk = PS()
                            k.mm([(p[:, 0:bn], [(wu[:, u, kc, :], hT[:, kc, b0:b0 + bn]) for kc in range(8)])],
                                 [wk] + hk(bi), [pk])
                            act(qT[:, 4 * h + u, b0:b0 + bn], p[:, 0:bn], AF.Copy, [pk], [('q', 4 * h + u, bi)],
                                scale=1.0 / 16.0)

            def attend(gi):
                g0, G = GROUPS[gi]
                for bi, (b0, bn) in enumerate(blocks(G)):
                    def S(hd, s):
                        for mc in range(2):
                            p, pk = PS()
                            k.mm([(p[:, 0:bn], [(kT[:, 2 * hd + dc, mc * 128:(mc + 1) * 128],
                                                 qT[:, 2 * hd + dc, b0:b0 + bn]) for dc in range(2)])],
                                 ['kT', ('q', 2 * hd, bi), ('q', 2 * hd + 1, bi)], [pk])
                            act(pT[s][:, mc, 0:bn], p[:, 0:bn], AF.Exp, [pk], [('pT', s, mc)])

                    def DO(hd, s):
                        pd, pdk = PS()
                        k.mm([(pd[:, 0:bn], [(ones1[:], pT[s][:, mc, 0:bn]) for mc in range(2)])],
                             ['ones', ('pT', s, 0), ('pT', s, 1)], [pdk])
                        act(rden[s][:, 0:bn], pd[:, 0:bn], AF.Ln, [pdk], [('rden', s)])
                        act(rden[s][:, 0:bn], rden[s][:, 0:bn], AF.Exp, [('rden', s)], [('rden', s)], scale=-1.0)

                    def O(hd, s):
                        for dc in range(2):
                            po, pok = PS()
                            k.mm([(po[:, 0:bn], [(vtok[:, mc, (2 * hd + dc) * 128:(2 * hd + dc + 1) * 128],
                                                  pT[s][:, mc, 0:bn]) for mc in range(2)])],
                                 ['vtok', ('pT', s, 0), ('pT', s, 1)], [pok])
                            tt(oT[:, 2 * hd + dc, b0:b0 + bn], po[:, 0:bn], rden[s][:, 0:bn], ALU.mult,
                               [pok, ('rden', s)], [('oT', 2 * hd + dc, bi)])
                    for hd in range(4):
                        S(hd, hd)
                    for hd in range(4):
                        DO(hd, hd)
                    for hd in range(4):
                        O(hd, hd)

            if not pre_done:
                prenorm(l, 0, 'xa_pre', sq, rs)
            for gi in range(3):
                qproj(gi)
                if gi == 0:
                    barrier()
                if gi < 2:
                    prenorm(l, gi + 1, 'xa_pre', sq, rs, lazy=True)
                elif nxt is not None:
                    prenorm(nxt[0], 0, nxt[1], sq, rs, lazy=True)
                attend(gi)
                proj_out(l, gi, [(('xo', h), 4) for h in range(2)], 8, oT,
                         lambda bi: [('oT', c, bi) for c in range(8)], outT)
                epilogue(l, gi, 'xa_post', 1.0, outT, sq, rs, lazy=(gi < 2))

        phases = []
        for l in range(nlayers):
            phases += [('ffn1', l), ('mix', l), ('xa', l), ('ffn2', l)]
        if stop_after is not None:
            phases = phases[:stop_after]
        PRE = {'ffn1': 'ffn1_pre', 'mix': 'mix_pre', 'xa': 'xa_pre', 'ffn2': 'ffn2_pre'}
        for i, (ph, l) in enumerate(phases):
            nxt = (phases[i + 1][1], PRE[phases[i + 1][0]]) if i + 1 < len(phases) else None
            pre_done = i > 0
            if ph == 'ffn1':
                ffn_phase(l, 'f1', 'ffn1_pre', 'ffn1_post', pre_done, nxt)
            elif ph == 'mix':
                mixer_phase(l, pre_done, nxt)
            elif ph == 'xa':
                xattn_phase(l, pre_done, nxt)
            else:
                ffn_phase(l, 'f2', 'ffn2_pre', 'ffn2_post', pre_done, nxt)

        flush_all()
        for gi_, (g0_, G_) in enumerate(GROUPS):
            for bi_, (b0_, bn_) in enumerate(blocks(G_)):
                B_ = gi_ * 2 + bi_
                t0_ = g0_ + b0_
                k.dma('sp', 'outs', y_d[:, :, t0_:t0_ + bn_], xT[:, :, t0_:t0_ + bn_],
                      reads=[('x', c, B_) for c in range(8)])
        k.wait_all('sp', ['outs'])
        print("program: instructions", k.nins, "waits", k.nwaits, "pe groups", k.cnt['pe'],
              "dve", k.cnt['dve'], "act", k.cnt['act'], "witems", wstate['n'], flush=True)
    return nc


def _unit(W, s, kc):
    return W[:, s * 128:(s + 1) * 128].reshape(kc, 128, 128).transpose(1, 0, 2).reshape(128, kc * 128)


def _wide(W, kc):
    return W.reshape(kc, 128, W.shape[1]).transpose(1, 0, 2).reshape(128, kc * W.shape[1])


def _pack_layer(inp, l):
    f = lambda n: np.asarray(inp[n][l], dtype=np.float32)
    out = np.empty((128, WLEN), np.float32)

    def put(name, arr):
        off, ln = ITEM_OFF[name]
        assert arr.shape == (128, ln), (name, arr.shape, ln)
        out[:, off:off + ln] = arr
    for pfx, nm in (('f1', 'ffn1'), ('f2', 'ffn2')):
        w1, w3, w2 = f(nm + '_w1'), f(nm + '_w3'), f(nm + '_w2')
        for it in range(11):
            put((pfx + '_13', it), np.concatenate(
                [_unit(w1, 2 * it, 8), _unit(w1, 2 * it + 1, 8), _unit(w3, 2 * it, 8), _unit(w3, 2 * it + 1, 8)], 1))
        for c in range(8):
            put((pfx + '_2', c), _unit(w2, c, NH))
    win = f('w_in')
    cu = lambda o: _unit(win, o // 128, 8)
    for j in range(4):
        put(('inA', j), np.concatenate([cu(j * 128), cu(512 + j * 128), cu(1024 + j * 128)], 1))
    put('inB', np.concatenate([cu(1536 + g * 128) for g in range(4)], 1))
    put('poolw', f('pool_w').transpose(1, 0, 2).reshape(128, 512))
    put('inCu', np.concatenate([cu(2048 + j * 128) for j in range(4)], 1))
    put('inCv', _wide(win[:, 2560:3072], 8))
    put('wsT', f('sgu_ws').transpose(2, 0, 1).reshape(128, 512))
    for jj in range(2):
        put(('inD', jj), np.concatenate([cu(3072 + (2 * jj) * 128), cu(3584 + (2 * jj) * 128),
                                         cu(3072 + (2 * jj + 1) * 128), cu(3584 + (2 * jj + 1) * 128)], 1))
    wg, wb = f('w_gate'), f('w_branch')
    for c in range(8):
        put(('gate', c), np.concatenate([_unit(wg, b * 8 + c, 8) for b in range(4)], 1))
        put(('branch', c), np.concatenate([_unit(wb[b], c, 4) for b in range(4)], 1))
    wo = f('w_o')
    for h in range(2):
        put(('wo', h), np.concatenate([_unit(wo, 4 * h + u, 8) for u in range(4)], 1))
    for nm, src in (('wk', 'xa_wk'), ('wq', 'xa_wq'), ('xo', 'xa_wo')):
        W = f(src)
        for h in range(2):
            put((nm, h), np.concatenate([_unit(W, 4 * h + u, 8) for u in range(4)], 1))
    wv = f('xa_wv')
    for h in range(2):
        put(('wv', h), _wide(wv[:, h * 512:(h + 1) * 512], 8))
    return out


def _pack_cols(inp):
    cols = np.zeros((128, DEPTH * NCOL), np.float32)

    def vec(v):
        v = np.asarray(v, np.float32)
        return v.reshape(-1, 128).T
    for l in range(DEPTH):
        b = l * NCOL
        for n in ('ffn1_pre', 'ffn1_post', 'mix_pre', 'mix_post', 'xa_pre', 'xa_post', 'ffn2_pre', 'ffn2_post'):
            cols[:, b + COL[n]:b + COL[n] + 8] = vec(inp[n + '_g'][l])
        cols[:, b + COL['mem_g']:b + COL['mem_g'] + 8] = vec(inp['mem_g'][l])
        cols[:, b + COL['b_gate']:b + COL['b_gate'] + 32] = vec(inp['b_gate'][l])
        cols[:, b + COL['conv_a_w']:b + COL['conv_a_w'] + 12] = vec(np.asarray(inp['conv_a_w'][l]).reshape(-1))
        cols[:, b + COL['conv_a_b']:b + COL['conv_a_b'] + 4] = vec(inp['conv_a_b'][l])
        cols[:, b + COL['pool_scale']:b + COL['pool_scale'] + 4] = vec(inp['pool_scale'][l])
        cols[:, b + COL['conv_d_w']:b + COL['conv_d_w'] + 124] = vec(np.asarray(inp['conv_d_w'][l]).reshape(-1))
        cols[:, b + COL['conv_d_b']:b + COL['conv_d_b'] + 4] = vec(inp['conv_d_b'][l])
        cols[:, b + COL['conv_d_ln_g']:b + COL['conv_d_ln_g'] + 4] = vec(inp['conv_d_ln_g'][l])
        cols[:, b + COL['conv_d_ln_b']:b + COL['conv_d_ln_b'] + 4] = vec(inp['conv_d_ln_b'][l])
    return cols


def _prepare(inp):
    x = np.asarray(inp['x'], np.float32)
    mem = np.asarray(inp['mem'], np.float32)
    wstream = np.stack([_pack_layer(inp, l) for l in range(DEPTH)], 0)
    cols = _pack_cols(inp)
    bc = np.empty((DEPTH, 128, 1536), np.float32)
    for l in range(DEPTH):
        row = np.concatenate([np.asarray(inp['sgu_ln_g'][l], np.float32), np.asarray(inp['sgu_ln_b'][l], np.float32),
                              np.asarray(inp['sgu_b'][l], np.float32).reshape(-1)])
        bc[l] = np.broadcast_to(row[None, :], (128, 1536))
    in_maps = []
    for core in range(NCORES):
        b, half = core // 2, core % 2
        xs = np.zeros((TOK, D), np.float32)
        if half == 0:
            xs[HALO:] = x[b, 0:2048]
        else:
            xs[:] = x[b, 2048 - HALO:4096]
        xTc = np.ascontiguousarray(xs.T.reshape(8, 128, TOK).transpose(1, 0, 2))
        memT = np.ascontiguousarray(mem[b].T.reshape(8, 128, NMEM).transpose(1, 0, 2)).reshape(128, 8 * NMEM)
        misc = np.zeros((128, 65 + 128), np.float32)
        misc[:, 65:] = np.eye(128, dtype=np.float32)
        misc[:, 0] = 0.0 if half == 0 else 1.0
        for g in range(4):
            win = 2 << g
            for t in range(16):
                misc[:, 1 + g * 16 + t] = (1.0 / min(t + 1, win)) if half == 0 else (1.0 / win)
        in_maps.append({"xT": xTc, "memT": memT, "wstream": wstream, "cols": cols, "bc": bc, "miscc": misc})
    return in_maps


def _gather(results):
    out = np.empty((4, SEQ, D), np.float32)
    for core in range(NCORES):
        b, half = core // 2, core % 2
        yT = np.asarray(results[core]["yT"])
        y = yT.transpose(1, 0, 2).reshape(D, TOK).T
        out[b, half * 2048:(half + 1) * 2048] = y[HALO:]
    return out


_NC_CACHE = {}


def kernel(**inputs):
    in_maps = _prepare(inputs)
    if 'nc' not in _NC_CACHE:
        _NC_CACHE['nc'] = build_program()
    res = run_bass_kernel_spmd(_NC_CACHE['nc'], in_maps, core_ids=list(range(NCORES)))
    return _gather(res.results)
```

```python
import contextlib
import numpy as np
import concourse.bass as bass
import concourse.mybir as mybir
from concourse.bass_utils import run_bass_kernel_spmd

F32 = mybir.dt.float32
BF16 = mybir.dt.bfloat16
AF = mybir.ActivationFunctionType
ALU = mybir.AluOpType

NCORES = 8
DEPTH = 4
D = 1024
SEQ = 4096
HALO = 128
TOK = 2048 + HALO
GROUPS = [(0, 768), (768, 768), (1536, 640)]
NMEM = 256
DFF = 2816
NH = DFF // 128
EPS = 1e-6
SLOT = 4096
NSLOT = 4


def blocks(G):
    return [(0, 384), (384, G - 384)]


COL = {}
_o = 0
for _n, _w in [('ffn1_pre', 8), ('ffn1_post', 8), ('mix_pre', 8), ('mix_post', 8), ('xa_pre', 8),
               ('xa_post', 8), ('mem_g', 8), ('ffn2_pre', 8), ('ffn2_post', 8), ('b_gate', 32),
               ('conv_a_w', 12), ('conv_a_b', 4), ('pool_scale', 4), ('conv_d_w', 124),
               ('conv_d_b', 4), ('conv_d_ln_g', 4), ('conv_d_ln_b', 4)]:
    COL[_n] = _o
    _o += _w
NCOL = _o

ITEMS = []
for _p in ('w', 'v'):
    pass


def _items():
    it = []
    it += [(('f1_13', i), 4096) for i in range(11)]
    it += [(('f1_2', c), 2816) for c in range(8)]
    it += [(('inA', j), 3072) for j in range(4)]
    it += [('inB', 4096), ('poolw', 512), ('inCu', 4096), ('inCv', 4096), ('wsT', 512)]
    it += [(('inD', j), 4096) for j in range(2)]
    for c in range(8):
        it += [(('gate', c), 4096), (('branch', c), 2048)]
    it += [(('wo', h), 4096) for h in range(2)]
    for n in ('wk', 'wv', 'wq', 'xo'):
        it += [((n, h), 4096) for h in range(2)]
    it += [(('f2_13', i), 4096) for i in range(11)]
    it += [(('f2_2', c), 2816) for c in range(8)]
    return it


ITEMS = _items()
ITEM_OFF = {}
_o = 0
for _n, _l in ITEMS:
    ITEM_OFF[_n] = (_o, _l)
    _o += _l
WLEN = _o


class Ctx:
    def __init__(self, nc, stack):
        self.nc = nc
        self.stack = stack
        self.eng = {'pe': nc.tensor, 'dve': nc.vector, 'act': nc.scalar,
                    'pool': nc.gpsimd, 'sp': nc.sync}
        self.sem = {}
        self.cnt = {}
        self.unordered = set()
        for e in self.eng:
            self.sem[e] = stack.enter_context(nc.semaphore("s_" + e))
            self.cnt[e] = 0
        self.known = {e: {} for e in self.eng}
        self.last_w = {}
        self.readers = {}
        self.nwaits = 0
        self.nins = 0
        self.psi = 0

    def new_sem(self, name, unordered=False):
        self.sem[name] = self.stack.enter_context(self.nc.semaphore(name))
        self.cnt[name] = 0
        if unordered:
            self.unordered.add(name)
        return name

    def sbuf(self, name, shape, dt):
        return self.stack.enter_context(self.nc.sbuf_tensor(name, list(shape), dt))

    def psum(self, name, shape, dt):
        return self.stack.enter_context(self.nc.psum_tensor(name, list(shape), dt))

    def _deps(self, reads, writes):
        deps = {}
        for k in reads:
            w = self.last_w.get(k)
            if w and deps.get(w[0], 0) < w[1]:
                deps[w[0]] = w[1]
        for k in writes:
            w = self.last_w.get(k)
            if w and deps.get(w[0], 0) < w[1]:
                deps[w[0]] = w[1]
            r = self.readers.get(k)
            if r:
                for s, v in r.items():
                    if deps.get(s, 0) < v:
                        deps[s] = v
        return deps

    def _emit_waits(self, e, deps):
        eng = self.eng[e]
        kn = self.known[e]
        for s, v in deps.items():
            if s == 'pe' and e == 'pe':
                continue
            if s in self.unordered:
                v = self.cnt[s]
            if kn.get(s, 0) >= v:
                continue
            eng.wait_ge(self.sem[s], v)
            kn[s] = v
            self.nwaits += 1

    def _record(self, s, v, reads, writes):
        for k in writes:
            self.last_w[k] = (s, v)
            self.readers[k] = {}
        for k in reads:
            r = self.readers.setdefault(k, {})
            if r.get(s, 0) < v:
                r[s] = v

    def op(self, e, fn, reads=(), writes=()):
        self._emit_waits(e, self._deps(reads, writes))
        ins = fn(self.eng[e])
        self.cnt[e] += 1
        ins.then_inc(self.sem[e], 1)
        self.nins += 1
        self._record(e, self.cnt[e], reads, writes)

    def dma(self, q, sem, out, in_, reads=(), writes=()):
        self._emit_waits(q, self._deps(reads, writes))
        ins = self.eng[q].dma_start(out=out, in_=in_)
        self.cnt[sem] += 16
        ins.then_inc(self.sem[sem], 16)
        self.nins += 1
        self._record(sem, self.cnt[sem], reads, writes)

    def mm(self, groups, reads, writes):
        self._emit_waits('pe', self._deps(reads, writes))
        ins = None
        for out, pairs in groups:
            n = len(pairs)
            for i, (l, r) in enumerate(pairs):
                ins = self.nc.tensor.matmul(out, l, r, start=(i == 0), stop=(i == n - 1))
                self.nins += 1
        self.cnt['pe'] += 1
        ins.then_inc(self.sem['pe'], 1)
        self._record('pe', self.cnt['pe'], reads, writes)

    def barrier(self):
        for e in ('dve', 'act'):
            self._emit_waits(e, {f: self.cnt[f] for f in ('pe', 'dve', 'act') if self.cnt[f] > 0})

    def wait_all(self, e, sems):
        self._emit_waits(e, {s: self.cnt[s] for s in sems if self.cnt[s] > 0})


class Arena:
    def __init__(self, t, words):
        self.t = t
        self.words = words
        self.off = 0

    def f32(self, n):
        assert self.off + n <= self.words, ("arena overflow", self.off, n, self.words)
        ap = self.t[:, self.off:self.off + n]
        self.off += n
        return ap

    def bf16(self, n):
        nw = (n + 1) // 2
        assert self.off + nw <= self.words, ("arena overflow", self.off, nw, self.words)
        ap = self.t[:, self.off:self.off + nw].bitcast(BF16)
        self.off += nw
        return ap[:, 0:n]


def build_program(nlayers=DEPTH, stop_after=None):
    nc = bass.Bass("TRN2", target_bir_lowering=False)
    x_d = nc.dram_tensor("xT", [128, 8, TOK], F32, kind="ExternalInput").ap()
    mem_d = nc.dram_tensor("memT", [128, 8 * NMEM], F32, kind="ExternalInput").ap()
    ws_d = nc.dram_tensor("wstream", [DEPTH, 128, WLEN], F32, kind="ExternalInput").ap()
    cols_d = nc.dram_tensor("cols", [128, DEPTH * NCOL], F32, kind="ExternalInput").ap()
    bc_d = nc.dram_tensor("bc", [DEPTH, 128, 1536], F32, kind="ExternalInput").ap()
    misc_d = nc.dram_tensor("miscc", [128, 65 + 128], F32, kind="ExternalInput").ap()
    y_d = nc.dram_tensor("yT", [128, 8, TOK], F32, kind="ExternalOutput").ap()

    with contextlib.ExitStack() as st:
        k = Ctx(nc, st)
        xT = k.sbuf("xTs", [128, 8, TOK], F32)
        hT = k.sbuf("hTs", [128, 8, 768], BF16)
        slots = [k.sbuf(f"slot{i}", [128, SLOT], BF16) for i in range(NSLOT)]
        ssem = [k.new_sem(f"ws{i}") for i in range(NSLOT)]
        k.new_sem("misc", unordered=True)
        k.new_sem("outs", unordered=True)
        cols = k.sbuf("cols_s", [128, DEPTH * NCOL], F32)
        bct = k.sbuf("bc_s", [128, 1536], F32)
        miscc = k.sbuf("misc_s", [128, 65 + 128], F32)
        onesD = k.sbuf("onesD", [128, 128], BF16)
        ones512 = k.sbuf("ones512", [128, 128], BF16)
        ones1 = k.sbuf("ones1", [128, 128], BF16)
        carA = k.sbuf("carA", [128, 4, 2], F32)
        carB = k.sbuf("carB", [128, 4, 15], F32)
        carD = k.sbuf("carD", [128, 4, 30], BF16)
        kT = k.sbuf("kTs", [128, 8, NMEM], BF16)
        vtok = k.sbuf("vtok", [128, 2, D], BF16)
        small = k.sbuf("small", [128, 32], F32)
        AW = (nc.sbuf_bytes_remaining - 1024) // 4
        arena_t = k.sbuf("arena", [128, AW], F32)
        ar = Arena(arena_t, AW)
        ps = [k.psum(f"ps{i}", [128, 512], F32) for i in range(8)]

        pend = {'q': [], 'busy': False}

        def run_one():
            due, fn = pend['q'].pop(0)
            pend['busy'] = True
            fn()
            pend['busy'] = False

        def defer(fn, delay):
            pend['q'].append((k.psi + delay, fn))

        def flush_all():
            assert not pend['busy']
            while pend['q']:
                run_one()

        def barrier():
            flush_all()
            k.barrier()

        def PS():
            if not pend['busy']:
                while pend['q'] and pend['q'][0][0] <= k.psi:
                    run_one()
            i = k.psi % 8
            k.psi += 1
            return ps[i], ('ps', i)

        def col(l, name, idx):
            o = l * NCOL + COL[name] + idx
            return cols[:, o:o + 1]

        def act(out, in_, func, reads, writes, **kw):
            k.op('act', lambda e: e.activation(out=out, in_=in_, func=func, **kw), reads, writes)

        def tt(out, in0, in1, op, reads, writes):
            k.op('dve', lambda e: e.tensor_tensor(out=out, in0=in0, in1=in1, op=op), reads, writes)

        def ts(out, in0, s1, s2, op0, op1, reads, writes):
            if s2 is None:
                k.op('dve', lambda e: e.tensor_scalar(out=out, in0=in0, scalar1=s1, scalar2=None, op0=op0),
                     reads, writes)
            else:
                k.op('dve', lambda e: e.tensor_scalar(out=out, in0=in0, scalar1=s1, scalar2=s2, op0=op0, op1=op1),
                     reads, writes)

        def stt(out, in0, scalar, in1, op0, op1, reads, writes):
            k.op('dve', lambda e: e.scalar_tensor_tensor(out=out, in0=in0, scalar=scalar, in1=in1, op0=op0, op1=op1),
                 reads, writes)

        def recip(out, in_, reads, writes):
            k.op('dve', lambda e: e.reciprocal(out=out, in_=in_), reads, writes)

        def vcopy(out, in_, reads, writes):
            k.op('dve', lambda e: e.tensor_copy(out=out, in_=in_), reads, writes)

        def vmemset(ap, val, writes):
            k.op('dve', lambda e: e.memset(ap, val), (), writes)

        wstate = {'n': 0}

        def witem(l, name):
            off, ln = ITEM_OFF[name]
            s = wstate['n'] % NSLOT
            wstate['n'] += 1
            k.dma('pool', ssem[s], slots[s][:, 0:ln], ws_d[l, :, off:off + ln], writes=[('slot', s)])
            return slots[s][:, 0:ln], ('slot', s)

        def units(ap, nu, kc):
            return ap.rearrange("p (u k n) -> p u k n", u=nu, k=kc, n=128)

        k.dma('sp', 'misc', cols[:, :], cols_d[:, :], writes=['cols'])
        k.dma('sp', 'misc', miscc[:, :], misc_d[:, :], writes=['miscc'])
        xsem = [k.new_sem(f"xl{B}") for B in range(6)]
        for gi_, (g0_, G_) in enumerate(GROUPS):
            for bi_, (b0_, bn_) in enumerate(blocks(G_)):
                B_ = gi_ * 2 + bi_
                t0_ = g0_ + b0_
                k.dma('sp', xsem[B_], xT[:, :, t0_:t0_ + bn_], x_d[:, :, t0_:t0_ + bn_],
                      writes=[('x', c, B_) for c in range(8)])
        vmemset(onesD[:], 1.0 / D, ['ones'])
        vmemset(ones512[:], 1.0 / 512, ['ones'])
        vmemset(ones1[:], 1.0, ['ones'])
        hmask = miscc[:, 0:1]
        pinv = miscc[:, 1:65].rearrange("p (g t) -> p g t", g=4)
        ident = miscc[:, 65:193]

        def arena_reset():
            barrier()
            ar.off = 0

        def squares(srcs, n, sq, rkeys):
            for c, s_ in enumerate(srcs):
                act(sq[:, c, 0:n], s_, AF.Square, [rkeys[c]], [('sq', c)])

        def rstd_from_sq(n, sq, rs, tag):
            p, pk = PS()
            k.mm([(p[:, 0:n], [(onesD[:], sq[:, c, 0:n]) for c in range(8)])],
                 ['ones'] + [('sq', c) for c in range(8)], [pk])
            act(rs[:, 0:n], p[:, 0:n], AF.Ln, [pk], [('rs', tag)], bias=EPS, scale=1.0)
            act(rs[:, 0:n], rs[:, 0:n], AF.Exp, [('rs', tag)], [('rs', tag)], scale=-0.5)

        def stats_rstd(srcs, n, sq, rs, rkeys, tag):
            squares(srcs, n, sq, rkeys)
            rstd_from_sq(n, sq, rs, tag)

        def norm_chain(BLK, srcs_fn, rkeys_fn, apply_fn, sq, rs, lazy):
            flush_all()
            squares(srcs_fn(0), BLK[0][1], sq, rkeys_fn(0))

            def stage(bi):
                def fn():
                    rstd_from_sq(BLK[bi][1], sq, rs, 'r')
                    apply_fn(bi)
                    if bi + 1 < len(BLK):
                        squares(srcs_fn(bi + 1), BLK[bi + 1][1], sq, rkeys_fn(bi + 1))
                        defer(stage(bi + 1), 4)
                return fn
            defer(stage(0), 4)
            if not lazy:
                flush_all()

        def prenorm(l, gi, gname, sq, rs, lazy=False):
            g0, G = GROUPS[gi]
            BLK = blocks(G)

            def apply(bi):
                b0, bn = BLK[bi]
                B = gi * 2 + bi
                t0 = g0 + b0
                for c in range(8):
                    stt(hT[:, c, b0:b0 + bn], xT[:, c, t0:t0 + bn], col(l, gname, c), rs[:, 0:bn],
                        ALU.mult, ALU.mult, [('x', c, B), ('rs', 'r'), 'cols'], [('h', c, bi)])
            norm_chain(BLK, lambda bi: [xT[:, c, g0 + BLK[bi][0]:g0 + BLK[bi][0] + BLK[bi][1]] for c in range(8)],
                       lambda bi: [('x', c, gi * 2 + bi) for c in range(8)], apply, sq, rs, lazy)

        def epilogue(l, gi, gname, coef, outT, sq, rs, lazy=False):
            g0, G = GROUPS[gi]
            BLK = blocks(G)

            def apply(bi):
                b0, bn = BLK[bi]
                B = gi * 2 + bi
                t0 = g0 + b0
                for c in range(8):
                    stt(outT[:, c, b0:b0 + bn], outT[:, c, b0:b0 + bn], col(l, gname, c), rs[:, 0:bn],
                        ALU.mult, ALU.mult, [('o', c, bi), ('rs', 'r'), 'cols'], [('o', c, bi)])
                    stt(xT[:, c, t0:t0 + bn], outT[:, c, b0:b0 + bn], coef, xT[:, c, t0:t0 + bn],
                        ALU.mult, ALU.add, [('o', c, bi), ('x', c, B)], [('x', c, B)])
            norm_chain(BLK, lambda bi: [outT[:, c, BLK[bi][0]:BLK[bi][0] + BLK[bi][1]] for c in range(8)],
                       lambda bi: [('o', c, bi) for c in range(8)], apply, sq, rs, lazy)

        def proj_out(l, gi, items, nk, src, src_keys, outT):
            g0, G = GROUPS[gi]
            cp = 0
            for name, nu in items:
                w, wk = witem(l, name)
                wu = units(w, nu, nk)
                for u in range(nu):
                    for bi, (b0, bn) in enumerate(blocks(G)):
                        p, pk = PS()
                        k.mm([(p[:, 0:bn], [(wu[:, u, kc, :], src[:, kc, b0:b0 + bn]) for kc in range(nk)])],
                             [wk] + src_keys(bi), [pk])
                        act(outT[:, cp, b0:b0 + bn], p[:, 0:bn], AF.Copy, [pk], [('o', cp, bi)])
                    cp += 1

        hk = lambda bi: [('h', c, bi) for c in range(8)]
        GM = 768

        def ffn_phase(l, pfx, pre, post, pre_done, nxt):
            arena_reset()
            gT = ar.bf16(NH * GM).rearrange("p (c t) -> p c t", c=NH)
            outT = ar.f32(8 * GM).rearrange("p (c t) -> p c t", c=8)
            sq = ar.bf16(8 * 384).rearrange("p (c t) -> p c t", c=8)
            rs = ar.f32(384)
            sa = [ar.f32(384), ar.f32(384)]
            cnt = {'n': 0}

            def s1(gi):
                g0, G = GROUPS[gi]
                for it in range(11):
                    if it == 1:
                        flush_all()
                    w, wk = witem(l, (pfx + '_13', it))
                    wu = units(w, 4, 8)
                    for u in range(2):
                        j = 2 * it + u
                        for bi, (b0, bn) in enumerate(blocks(G)):
                            pa, pak = PS()
                            k.mm([(pa[:, 0:bn], [(wu[:, u, kc, :], hT[:, kc, b0:b0 + bn]) for kc in range(8)])],
                                 [wk] + hk(bi), [pak])
                            s = cnt['n'] % 2
                            cnt['n'] += 1
                            act(sa[s][:, 0:bn], pa[:, 0:bn], AF.Silu, [pak], [('sa', s)])
                            pb, pbk = PS()
                            k.mm([(pb[:, 0:bn], [(wu[:, 2 + u, kc, :], hT[:, kc, b0:b0 + bn]) for kc in range(8)])],
                                 [wk] + hk(bi), [pbk])
                            tt(gT[:, j, b0:b0 + bn], pb[:, 0:bn], sa[s][:, 0:bn], ALU.mult,
                               [pbk, ('sa', s)], [('g', j, bi)])

            if not pre_done:
                prenorm(l, 0, pre, sq, rs)
            for gi in range(3):
                s1(gi)
                if gi < 2:
                    prenorm(l, gi + 1, pre, sq, rs, lazy=True)
                elif nxt is not None:
                    prenorm(nxt[0], 0, nxt[1], sq, rs, lazy=True)
                proj_out(l, gi, [((pfx + '_2', c), 1) for c in range(8)], NH, gT,
                         lambda bi: [('g', j, bi) for j in range(NH)], outT)
                epilogue(l, gi, post, 0.5, outT, sq, rs, lazy=(gi < 2))

        def mixer_phase(l, pre_done, nxt):
            arena_reset()
            k.dma('sp', k.new_sem(f"bc{l}"), bct[:, :], bc_d[l, :, :], writes=['bc'])
            sbr = [ar.bf16(4 * GM).rearrange("p (c t) -> p c t", c=4) for _ in range(4)]
            merged = ar.bf16(8 * GM).rearrange("p (c t) -> p c t", c=8)
            sq = ar.bf16(8 * 384).rearrange("p (c t) -> p c t", c=8)
            rs = ar.f32(384)
            mark = ar.off
            if not pre_done:
                prenorm(l, 0, 'mix_pre', sq, rs)
            for gi in range(3):
                mixer_group(l, gi, sbr, merged, sq, rs, mark, nxt)

        def mixer_group(l, gi, sbr, merged, sq, rs, mark, nxt):
            g0, G = GROUPS[gi]
            BL = blocks(G)

            def zmm(wu, u, bi, wk):
                b0, bn = BL[bi]
                p, pk = PS()
                k.mm([(p[:, 0:bn], [(wu[:, u, kc, :], hT[:, kc, b0:b0 + bn]) for kc in range(8)])],
                     [wk] + hk(bi), [pk])
                return p, pk

            def carry(buf, H, car, j, key, ckey):
                if gi == 0:
                    vmemset(buf[:, 0:H], 0.0, [key])
                    ts(buf[:, H:H + HALO], buf[:, H:H + HALO], hmask, None, ALU.mult, None,
                       [key, 'miscc'], [key])
                else:
                    vcopy(buf[:, 0:H], car[:, j, :], [('car', ckey, j)], [key])
                if gi < 2:
                    vcopy(car[:, j, :], buf[:, G:G + H], [key], [('car', ckey, j)])

            ar.off = mark + 0
            barrier()
            xa_sb = [ar.f32(GM), ar.f32(GM)]
            cxs = [ar.f32(2 + GM), ar.f32(2 + GM)]
            accs = [ar.f32(GM), ar.f32(GM)]
            sA = sbr[0]
            for j in range(4):
                d = j % 2
                xs, cx, acc = xa_sb[d], cxs[d], accs[d]
                kx_, kc_, ka_ = ('xa_sb', d), ('cx', d), ('accA', d)
                w, wk = witem(l, ('inA', j))
                wu = units(w, 3, 8)
                for bi, (b0, bn) in enumerate(BL):
                    p, pk = zmm(wu, 0, bi, wk)
                    act(xs[:, b0:b0 + bn], p[:, 0:bn], AF.Copy, [pk], [kx_])
                    p2, pk2 = zmm(wu, 2, bi, wk)
                    tt(cx[:, 2 + b0:2 + b0 + bn], p2[:, 0:bn], xs[:, b0:b0 + bn], ALU.mult,
                       [pk2, kx_], [kc_])
                carry(cx, 2, carA, j, kc_, 'A')
                act(acc[:, 0:G], cx[:, 2:2 + G], AF.Identity, [kc_, 'cols'], [ka_],
                    scale=col(l, 'conv_a_w', 2 * 4 + j), bias=col(l, 'conv_a_b', j))
                stt(acc[:, 0:G], cx[:, 1:1 + G], col(l, 'conv_a_w', 1 * 4 + j), acc[:, 0:G], ALU.mult, ALU.add,
                    [kc_, ka_], [ka_])
                stt(acc[:, 0:G], cx[:, 0:G], col(l, 'conv_a_w', 0 * 4 + j), acc[:, 0:G], ALU.mult, ALU.add,
                    [kc_, ka_], [ka_])
                for bi, (b0, bn) in enumerate(BL):
                    p, pk = zmm(wu, 1, bi, wk)
                    tt(sA[:, j, b0:b0 + bn], p[:, 0:bn], acc[:, b0:b0 + bn], ALU.mult, [pk, ka_],
                       [('s', 0, j, bi)])

            ar.off = mark + 0
            barrier()
            pbufs = [ar.f32(15 + GM), ar.f32(15 + GM)]
            tas = [ar.f32(15 + GM), ar.f32(15 + GM)]
            tbs = [ar.f32(15 + GM), ar.f32(15 + GM)]
            pooleds = [ar.bf16(GM), ar.bf16(GM)]
            t16 = ar.f32(32)
            sB = sbr[1]
            w, wk = witem(l, 'inB')
            wu = units(w, 4, 8)
            pw, pwk = witem(l, 'poolw')
            pwu = pw.rearrange("p (g d) -> p g d", g=4)
            L = 15 + G
            for pair in ((0, 1), (2, 3)):
                for g in pair:
                    d = g % 2
                    for bi, (b0, bn) in enumerate(BL):
                        p, pk = zmm(wu, g, bi, wk)
                        act(pbufs[d][:, 15 + b0:15 + b0 + bn], p[:, 0:bn], AF.Copy, [pk], [('pbuf', d)])
                for g in pair:
                    d = g % 2
                    carry(pbufs[d], 15, carB, g, ('pbuf', d), 'B')
                cur = {}
                for g in pair:
                    d = g % 2
                    tt(tas[d][:, 1:L], pbufs[d][:, 1:L], pbufs[d][:, 0:L - 1], ALU.add, [('pbuf', d)], [('ta', d)])
                    cur[g] = (tas[d], ('ta', d), tbs[d], ('tb', d), 1)
                for step in range(3):
                    for g in pair:
                        if step < g:
                            c_, ck_, o_, ok_, sh = cur[g]
                            sh2 = sh * 2
                            lo = 2 * sh2 - 1
                            tt(o_[:, lo:L], c_[:, lo:L], c_[:, lo - sh2:L - sh2], ALU.add, [ck_], [ok_])
                            cur[g] = (o_, ok_, c_, ck_, sh2)
                for g in pair:
                    d = g % 2
                    win = 2 << g
                    c_, ck_ = cur[g][0], cur[g][1]
                    stt(pooleds[d][:, 0:G], c_[:, 15:L], 1.0 / win, pbufs[d][:, 15:L], ALU.mult, ALU.subtract,
                        [ck_, ('pbuf', d)], [('pooled', d)])
                    if gi == 0:
                        tt(t16[:, d * 16:d * 16 + 16], c_[:, 15 + HALO:15 + HALO + 16], pinv[:, g, :], ALU.mult,
                           [ck_, 'miscc'], [('t16', d)])
                        tt(pooleds[d][:, HALO:HALO + 16], t16[:, d * 16:d * 16 + 16],
                           pbufs[d][:, 15 + HALO:15 + HALO + 16], ALU.subtract,
                           [('t16', d), ('pbuf', d), ('pooled', d)], [('pooled', d)])
                def fin(pair=pair):
                    for g in pair:
                        d = g % 2
                        for bi, (b0, bn) in enumerate(BL):
                            p, pk = PS()
                            k.mm([(p[:, 0:bn], [(pwu[:, g, :], pooleds[d][:, b0:b0 + bn])])],
                                 [pwk, ('pooled', d)], [pk])
                            act(sB[:, g, b0:b0 + bn], p[:, 0:bn], AF.Copy, [pk, 'cols'], [('s', 1, g, bi)],
                                scale=col(l, 'pool_scale', g))
                if pair == (0, 1):
                    defer(fin, 3)
                else:
                    fin()

            ar.off = mark + 0
            barrier()
            uT = ar.bf16(4 * GM).rearrange("p (c t) -> p c t", c=4)
            vts = [ar.f32(512) for _ in range(3)]
            vns = [ar.bf16(512) for _ in range(3)]
            gscs = [ar.f32(512) for _ in range(3)]
            sC = sbr[2]

            def gelu_tanh(out, p, sc, pk, wkeys, sk):
                act(sc, p, AF.Square, [pk], [sk], scale=0.21145921592590706)
                stt(sc, sc, 1.0, p, ALU.add, ALU.mult, [sk, pk], [sk])
                act(sc, sc, AF.Sigmoid, [sk], [sk], scale=1.5957691216057308)
                tt(out, sc, p, ALU.mult, [sk, pk], wkeys)

            w, wk = witem(l, 'inCu')
            wu = units(w, 4, 8)
            nu_ = 0
            for j in range(4):
                for bi, (b0, bn) in enumerate(BL):
                    p, pk = zmm(wu, j, bi, wk)
                    d = nu_ % 2
                    nu_ += 1
                    gelu_tanh(uT[:, j, b0:b0 + bn], p[:, 0:bn], gscs[d][:, 0:bn], pk, [('u', j, bi)], ('gsc', d))
            wv_, wvk = witem(l, 'inCv')
            wvu = wv_.rearrange("p (k n) -> p k n", k=8)
            wst, wstk = witem(l, 'wsT')
            wsu = wst.rearrange("p (g i) -> p g i", g=4)
            vmemset(wsu[64:128, :, 0:64], 0.0, [wstk])
            lng = bct[:, 0:512]
            lnb = bct[:, 512:1024]
            sgb = bct[:, 1024:1536]
            ntile = G // 128
            for t0_ in range(0, ntile, 3):
                tiles = list(range(t0_, min(t0_ + 3, ntile)))
                pp = {}
                for i, tti in enumerate(tiles):
                    tk0 = tti * 128
                    bi = 0 if tk0 < 384 else 1
                    p, pk = PS()
                    k.mm([(p[:, :], [(hT[:, kc, tk0:tk0 + 128], wvu[:, kc, :]) for kc in range(8)])],
                         [wvk] + hk(bi), [pk])
                    pp[i] = (p, pk)
                for i in pp:
                    act(gscs[i][:, :], pp[i][0][:, :], AF.Square, [pp[i][1]], [('gsc', i)], scale=0.21145921592590706)
                for i in pp:
                    stt(gscs[i][:, :], gscs[i][:, :], 1.0, pp[i][0][:, :], ALU.add, ALU.mult,
                        [('gsc', i), pp[i][1]], [('gsc', i)])
                for i in pp:
                    act(gscs[i][:, :], gscs[i][:, :], AF.Sigmoid, [('gsc', i)], [('gsc', i)], scale=1.5957691216057308)
                for i in pp:
                    tt(vts[i][:, :], gscs[i][:, :], pp[i][0][:, :], ALU.mult, [('gsc', i), pp[i][1]], [('vt', i)])
                for i in pp:
                    o = 10 * i
                    k.op('dve', lambda e: e.bn_stats(out=small[:, o:o + 6], in_=vts[i][:, :]), [('vt', i)], [('bst', i)])
                for i in pp:
                    o = 10 * i
                    k.op('dve', lambda e: e.bn_aggr(out=small[:, o + 6:o + 8], in_=small[:, o:o + 6]),
                         [('bst', i)], [('bag', i)])
                for i in pp:
                    o = 10 * i
                    act(small[:, o + 8:o + 9], small[:, o + 7:o + 8], AF.Sqrt, [('bag', i)], [('sd', i)],
                        bias=EPS, scale=1.0)
                for i in pp:
                    o = 10 * i
                    recip(small[:, o + 8:o + 9], small[:, o + 8:o + 9], [('sd', i)], [('sd', i)])
                for i in pp:
                    o = 10 * i
                    stt(small[:, o + 9:o + 10], small[:, o + 6:o + 7], -1.0, small[:, o + 8:o + 9],
                        ALU.mult, ALU.mult, [('bag', i), ('sd', i)], [('nmr', i)])
                for i in pp:
                    o = 10 * i
                    act(vts[i][:, :], vts[i][:, :], AF.Identity, [('vt', i), ('sd', i), ('nmr', i)], [('vt', i)],
                        scale=small[:, o + 8:o + 9], bias=small[:, o + 9:o + 10])
                for i in pp:
                    tt(vts[i][:, :], vts[i][:, :], lng, ALU.mult, [('vt', i), 'bc'], [('vt', i)])
                for i in pp:
                    tt(vns[i][:, :], vts[i][:, :], lnb, ALU.add, [('vt', i), 'bc'], [('vn', i)])
                p2s = {}
                for i in pp:
                    p2, pk2 = PS()
                    k.mm([(p2[:, g * 128:(g + 1) * 128], [(vns[i][:, g * 128:(g + 1) * 128], wsu[:, g, :])])
                          for g in range(4)], [('vn', i), wstk], [pk2])
                    p2s[i] = (p2, pk2)
                for i in pp:
                    tt(gscs[i][:, :], p2s[i][0][:, :], sgb, ALU.add, [p2s[i][1], 'bc'], [('gsc', i)])
                for i, tti in enumerate(tiles):
                    tk0 = tti * 128
                    bi = 0 if tk0 < 384 else 1
                    tt(sC[:, :, tk0:tk0 + 128], gscs[i][:, :].rearrange("p (g i) -> p g i", g=4),
                       uT[:, :, tk0:tk0 + 128], ALU.mult,
                       [('gsc', i)] + [('u', j, bi) for j in range(4)], [('s', 2, j, bi) for j in range(4)])

            ar.off = mark + 0
            barrier()
            yD = ar.f32(4 * GM).rearrange("p (c t) -> p c t", c=4)
            markD = ar.off
            sig = [ar.f32(384), ar.f32(384)]
            glus = [ar.bf16(32 + GM), ar.bf16(32 + GM)]
            dg = ar.bf16(31 * 128).rearrange("p (k n) -> p k n", k=31)
            sD = sbr[3]
            nsg = 0
            identb = ident[:, :].unsqueeze(1).broadcast_to([128, 31, 128])
            for jj in range(2):
                w, wk = witem(l, ('inD', jj))
                wu = units(w, 4, 8)
                for u in range(2):
                    j = 2 * jj + u
                    d = j % 2
                    glu = glus[d]
                    kg_ = ('glu', d)
                    o = l * NCOL + COL['conv_d_w'] + j
                    wtap = cols[:, o:o + 124:4].unsqueeze(2).broadcast_to([128, 31, 128])
                    tt(dg[:, :, :], identb, wtap, ALU.mult, ['miscc', 'cols'], ['dg'])
                    for bi, (b0, bn) in enumerate(BL):
                        p, pk = zmm(wu, 2 * u + 1, bi, wk)
                        s = nsg % 2
                        nsg += 1
                        act(sig[s][:, 0:bn], p[:, 0:bn], AF.Sigmoid, [pk], [('sig', s)])
                        p2, pk2 = zmm(wu, 2 * u, bi, wk)
                        tt(glu[:, 30 + b0:30 + b0 + bn], p2[:, 0:bn], sig[s][:, 0:bn], ALU.mult,
                           [pk2, ('sig', s)], [kg_])
                    carry(glu, 30, carD, j, kg_, 'D')
                    for bi, (b0, bn) in enumerate(BL):
                        p, pk = PS()
                        k.mm([(p[:, 0:bn], [(dg[:, kk, :], glu[:, kk + b0:kk + b0 + bn]) for kk in range(31)])],
                             ['dg', kg_], [pk])
                        act(yD[:, j, b0:b0 + bn], p[:, 0:bn], AF.Identity, [pk, 'cols'], [('yD', j, bi)],
                            bias=col(l, 'conv_d_b', j), scale=1.0)
            barrier()
            ar.off = markD
            ybf = ar.bf16(4 * 384).rearrange("p (c t) -> p c t", c=4)
            ysq = ar.bf16(4 * 384).rearrange("p (c t) -> p c t", c=4)
            mean_sb = ar.f32(384)
            var_sb = ar.f32(384)
            t1s = [ar.f32(384), ar.f32(384)]
            for bi, (b0, bn) in enumerate(BL):
                for j in range(4):
                    act(ybf[:, j, 0:bn], yD[:, j, b0:b0 + bn], AF.Copy, [('yD', j, bi)], [('ybf', j)])
                    act(ysq[:, j, 0:bn], yD[:, j, b0:b0 + bn], AF.Square, [('yD', j, bi)], [('ysq', j)])
                pm, pmk = PS()
                k.mm([(pm[:, 0:bn], [(ones512[:], ybf[:, j, 0:bn]) for j in range(4)])],
                     ['ones'] + [('ybf', j) for j in range(4)], [pmk])
                pq, pqk = PS()
                k.mm([(pq[:, 0:bn], [(ones512[:], ysq[:, j, 0:bn]) for j in range(4)])],
                     ['ones'] + [('ysq', j) for j in range(4)], [pqk])
                act(mean_sb[:, 0:bn], pm[:, 0:bn], AF.Copy, [pmk], ['mean_sb'])
                tt(var_sb[:, 0:bn], mean_sb[:, 0:bn], mean_sb[:, 0:bn], ALU.mult, ['mean_sb'], ['var_sb'])
                tt(var_sb[:, 0:bn], pq[:, 0:bn], var_sb[:, 0:bn], ALU.subtract, [pqk, 'var_sb'], ['var_sb'])
                ts(var_sb[:, 0:bn], var_sb[:, 0:bn], 0.0, None, ALU.max, None, ['var_sb'], ['var_sb'])
                act(var_sb[:, 0:bn], var_sb[:, 0:bn], AF.Ln, ['var_sb'], ['var_sb'], bias=EPS, scale=1.0)
                act(var_sb[:, 0:bn], var_sb[:, 0:bn], AF.Exp, ['var_sb'], ['var_sb'], scale=-0.5)
                for j in range(4):
                    t1 = t1s[j % 2]
                    tk_ = ('t1', j % 2)
                    tt(t1[:, 0:bn], yD[:, j, b0:b0 + bn], mean_sb[:, 0:bn], ALU.subtract,
                       [('yD', j, bi), 'mean_sb'], [tk_])
                    tt(t1[:, 0:bn], t1[:, 0:bn], var_sb[:, 0:bn], ALU.mult, [tk_, 'var_sb'], [tk_])
                    act(sD[:, j, b0:b0 + bn], t1[:, 0:bn], AF.Silu, [tk_, 'cols'], [('s', 3, j, bi)],
                        scale=col(l, 'conv_d_ln_g', j), bias=col(l, 'conv_d_ln_b', j))

            ar.off = mark + 0
            barrier()
            outT = ar.f32(8 * GM).rearrange("p (c t) -> p c t", c=8)
            gsb = [ar.f32(384), ar.f32(384)]
            macc = ar.f32(384)
            prod1 = ar.f32(384)
            prod = [prod1, prod1]
            ng = 0
            for c in range(8):
                gw, gwk = witem(l, ('gate', c))
                gwu = units(gw, 4, 8)
                bw, bwk = witem(l, ('branch', c))
                bwu = units(bw, 4, 4)
                for bi, (b0, bn) in enumerate(BL):
                    for b in range(4):
                        p, pk = zmm(gwu, b, bi, gwk)
                        s = ng % 2
                        ng += 1
                        act(gsb[s][:, 0:bn], p[:, 0:bn], AF.Sigmoid, [pk, 'cols'], [('gsb', s)],
                            bias=col(l, 'b_gate', b * 8 + c), scale=1.0)
                        p2, pk2 = PS()
                        k.mm([(p2[:, 0:bn], [(bwu[:, b, kc, :], sbr[b][:, kc, b0:b0 + bn]) for kc in range(4)])],
                             [bwk] + [('s', b, kc, bi) for kc in range(4)], [pk2])
                        if b == 0:
                            tt(macc[:, 0:bn], p2[:, 0:bn], gsb[s][:, 0:bn], ALU.mult, [pk2, ('gsb', s)], ['macc'])
                        else:
                            pr, prk = prod[b % 2], ('prod', 0)
                            tt(pr[:, 0:bn], p2[:, 0:bn], gsb[s][:, 0:bn], ALU.mult, [pk2, ('gsb', s)], [prk])
                            if b < 3:
                                tt(macc[:, 0:bn], macc[:, 0:bn], pr[:, 0:bn], ALU.add, ['macc', prk], ['macc'])
                            else:
                                tt(merged[:, c, b0:b0 + bn], macc[:, 0:bn], pr[:, 0:bn], ALU.add,
                                   ['macc', prk], [('m', c, bi)])
            if gi < 2:
                prenorm(l, gi + 1, 'mix_pre', sq, rs, lazy=True)
            elif nxt is not None:
                prenorm(nxt[0], 0, nxt[1], sq, rs, lazy=True)
            proj_out(l, gi, [(('wo', h), 4) for h in range(2)], 8, merged,
                     lambda bi: [('m', c, bi) for c in range(8)], outT)
            epilogue(l, gi, 'mix_post', 1.0, outT, sq, rs)

        def xattn_phase(l, pre_done, nxt):
            arena_reset()
            qT = ar.bf16(8 * GM).rearrange("p (c t) -> p c t", c=8)
            oT = ar.bf16(8 * GM).rearrange("p (c t) -> p c t", c=8)
            kv0 = ar.off
            memt_flat = ar.f32(8 * NMEM)
            memt = memt_flat.rearrange("p (c m) -> p c m", c=8)
            msq = ar.bf16(8 * NMEM).rearrange("p (c m) -> p c m", c=8)
            memn = ar.bf16(8 * NMEM).rearrange("p (c m) -> p c m", c=8)
            mrs = ar.f32(NMEM)
            ar.off = kv0
            outT = ar.f32(8 * GM).rearrange("p (c t) -> p c t", c=8)
            sq = ar.bf16(8 * 384).rearrange("p (c t) -> p c t", c=8)
            rs = ar.f32(384)
            pT = [ar.bf16(2 * 384).rearrange("p (c t) -> p c t", c=2) for _ in range(4)]
            rden = [ar.f32(384) for _ in range(4)]
            k.wait_all('sp', ['pe', 'dve', 'act'])
            k.dma('sp', k.new_sem(f"mem{l}"), memt_flat, mem_d[:, :], writes=['memt'])
            squares([memt[:, c, :] for c in range(8)], NMEM, msq, ['memt'] * 8)
            mk = [('memn', c) for c in range(8)]

            def kv2():
                for h in range(2):
                    w, wk = witem(l, ('wk', h))
                    wu = units(w, 4, 8)
                    for u in range(4):
                        p, pk = PS()
                        k.mm([(p[:, 0:NMEM], [(wu[:, u, kc, :], memn[:, kc, :]) for kc in range(8)])],
                             [wk] + mk, [pk])
                        act(kT[:, 4 * h + u, :], p[:, 0:NMEM], AF.Copy, [pk], ['kT'])
                for h in range(2):
                    w, wk = witem(l, ('wv', h))
                    wu = w.rearrange("p (k n) -> p k n", k=8)
                    for mc in range(2):
                        p, pk = PS()
                        k.mm([(p[:, :], [(memn[:, kc, mc * 128:(mc + 1) * 128], wu[:, kc, :]) for kc in range(8)])],
                             [wk] + mk, [pk])
                        act(vtok[:, mc, h * 512:(h + 1) * 512], p[:, :], AF.Copy, [pk], ['vtok'])

            def kv1():
                rstd_from_sq(NMEM, msq, mrs, 'mem')
                for c in range(8):
                    stt(memn[:, c, :], memt[:, c, :], col(l, 'mem_g', c), mrs[:, :], ALU.mult, ALU.mult,
                        ['memt', ('rs', 'mem'), 'cols'], [('memn', c)])
                defer(kv2, 4)
            defer(kv1, 4)
            na = {'n': 0}

            def qproj(gi):
                g0, G = GROUPS[gi]
                for h in range(2):
                    if h == 1:
                        flush_all()
                    w, wk = witem(l, ('wq', h))
                    wu = units(w, 4, 8)
                    for u in range(4):
                        for bi, (b0, bn) in enumerate(blocks(G)):
                            p, pk = PS()
                            k.mm([(p[:, 0:bn], [(wu[:, u, kc, :], hT[:, kc, b0:b0 + bn]) for kc in range(8)])],
                                 [wk] + hk(bi), [pk])
                            act(qT[:, 4 * h + u, b0:b0 + bn], p[:, 0:bn], AF.Copy, [pk], [('q', 4 * h + u, bi)],
                                scale=1.0 / 16.0)

            def attend(gi):
                g0, G = GROUPS[gi]
                for bi, (b0, bn) in enumerate(blocks(G)):
                    def S(hd, s):
                        for mc in range(2):
                            p, pk = PS()
                            k.mm([(p[:, 0:bn], [(kT[:, 2 * hd + dc, mc * 128:(mc + 1) * 128],
                                                 qT[:, 2 * hd + dc, b0:b0 + bn]) for dc in range(2)])],
                                 ['kT', ('q', 2 * hd, bi), ('q', 2 * hd + 1, bi)], [pk])
                            act(pT[s][:, mc, 0:bn], p[:, 0:bn], AF.Exp, [pk], [('pT', s, mc)])

                    def DO(hd, s):
                        pd, pdk = PS()
                        k.mm([(pd[:, 0:bn], [(ones1[:], pT[s][:, mc, 0:bn]) for mc in range(2)])],
                             ['ones', ('pT', s, 0), ('pT', s, 1)], [pdk])
                        act(rden[s][:, 0:bn], pd[:, 0:bn], AF.Ln, [pdk], [('rden', s)])
                        act(rden[s][:, 0:bn], rden[s][:, 0:bn], AF.Exp, [('rden', s)], [('rden', s)], scale=-1.0)

                    def O(hd, s):
                        for dc in range(2):
                            po, pok = PS()
                            k.mm([(po[:, 0:bn], [(vtok[:, mc, (2 * hd + dc) * 128:(2 * hd + dc + 1) * 128],
                                                  pT[s][:, mc, 0:bn]) for mc in range(2)])],
                                 ['vtok', ('pT', s, 0), ('pT', s, 1)], [pok])
                            tt(oT[:, 2 * hd + dc, b0:b0 + bn], po[:, 0:bn], rden[s][:, 0:bn], ALU.mult,
                               [pok, ('rden', s)], [('oT', 2 * hd + dc, bi)])
                    for hd in range(4):
                        S(hd, hd)
                    for hd in range(4):
                        DO(hd, hd)
                    for hd in range(4):
                        O(hd, hd)

            if not pre_done:
                prenorm(l, 0, 'xa_pre', sq, rs)
            for gi in range(3):
                qproj(gi)
                if gi == 0:
                    barrier()
                if gi < 2:
                    prenorm(l, gi + 1, 'xa_pre', sq, rs, lazy=True)
                elif nxt is not None:
                    prenorm(nxt[0], 0, nxt[1], sq, rs, lazy=True)
                attend(gi)
                proj_out(l, gi, [(('xo', h), 4) for h in range(2)], 8, oT,
                         lambda bi: [('oT', c, bi) for c in range(8)], outT)
                epilogue(l, gi, 'xa_post', 1.0, outT, sq, rs, lazy=(gi < 2))

        phases = []
        for l in range(nlayers):
            phases += [('ffn1', l), ('mix', l), ('xa', l), ('ffn2', l)]
        if stop_after is not None:
            phases = phases[:stop_after]
        PRE = {'ffn1': 'ffn1_pre', 'mix': 'mix_pre', 'xa': 'xa_pre', 'ffn2': 'ffn2_pre'}
        for i, (ph, l) in enumerate(phases):
            nxt = (phases[i + 1][1], PRE[phases[i + 1][0]]) if i + 1 < len(phases) else None
            pre_done = i > 0
            if ph == 'ffn1':
                ffn_phase(l, 'f1', 'ffn1_pre', 'ffn1_post', pre_done, nxt)
            elif ph == 'mix':
                mixer_phase(l, pre_done, nxt)
            elif ph == 'xa':
                xattn_phase(l, pre_done, nxt)
            else:
                ffn_phase(l, 'f2', 'ffn2_pre', 'ffn2_post', pre_done, nxt)

        flush_all()
        for gi_, (g0_, G_) in enumerate(GROUPS):
            for bi_, (b0_, bn_) in enumerate(blocks(G_)):
                B_ = gi_ * 2 + bi_
                t0_ = g0_ + b0_
                k.dma('sp', 'outs', y_d[:, :, t0_:t0_ + bn_], xT[:, :, t0_:t0_ + bn_],
                      reads=[('x', c, B_) for c in range(8)])
        k.wait_all('sp', ['outs'])
        print("program: instructions", k.nins, "waits", k.nwaits, "pe groups", k.cnt['pe'],
              "dve", k.cnt['dve'], "act", k.cnt['act'], "witems", wstate['n'], flush=True)
    return nc


def _unit(W, s, kc):
    return W[:, s * 128:(s + 1) * 128].reshape(kc, 128, 128).transpose(1, 0, 2).reshape(128, kc * 128)


def _wide(W, kc):
    return W.reshape(kc, 128, W.shape[1]).transpose(1, 0, 2).reshape(128, kc * W.shape[1])


def _pack_layer(inp, l):
    f = lambda n: np.asarray(inp[n][l], dtype=np.float32)
    out = np.empty((128, WLEN), np.float32)

    def put(name, arr):
        off, ln = ITEM_OFF[name]
        assert arr.shape == (128, ln), (name, arr.shape, ln)
        out[:, off:off + ln] = arr
    for pfx, nm in (('f1', 'ffn1'), ('f2', 'ffn2')):
        w1, w3, w2 = f(nm + '_w1'), f(nm + '_w3'), f(nm + '_w2')
        for it in range(11):
            put((pfx + '_13', it), np.concatenate(
                [_unit(w1, 2 * it, 8), _unit(w1, 2 * it + 1, 8), _unit(w3, 2 * it, 8), _unit(w3, 2 * it + 1, 8)], 1))
        for c in range(8):
            put((pfx + '_2', c), _unit(w2, c, NH))
    win = f('w_in')
    cu = lambda o: _unit(win, o // 128, 8)
    for j in range(4):
        put(('inA', j), np.concatenate([cu(j * 128), cu(512 + j * 128), cu(1024 + j * 128)], 1))
    put('inB', np.concatenate([cu(1536 + g * 128) for g in range(4)], 1))
    put('poolw', f('pool_w').transpose(1, 0, 2).reshape(128, 512))
    put('inCu', np.concatenate([cu(2048 + j * 128) for j in range(4)], 1))
    put('inCv', _wide(win[:, 2560:3072], 8))
    put('wsT', f('sgu_ws').transpose(2, 0, 1).reshape(128, 512))
    for jj in range(2):
        put(('inD', jj), np.concatenate([cu(3072 + (2 * jj) * 128), cu(3584 + (2 * jj) * 128),
                                         cu(3072 + (2 * jj + 1) * 128), cu(3584 + (2 * jj + 1) * 128)], 1))
    wg, wb = f('w_gate'), f('w_branch')
    for c in range(8):
        put(('gate', c), np.concatenate([_unit(wg, b * 8 + c, 8) for b in range(4)], 1))
        put(('branch', c), np.concatenate([_unit(wb[b], c, 4) for b in range(4)], 1))
    wo = f('w_o')
    for h in range(2):
        put(('wo', h), np.concatenate([_unit(wo, 4 * h + u, 8) for u in range(4)], 1))
    for nm, src in (('wk', 'xa_wk'), ('wq', 'xa_wq'), ('xo', 'xa_wo')):
        W = f(src)
        for h in range(2):
            put((nm, h), np.concatenate([_unit(W, 4 * h + u, 8) for u in range(4)], 1))
    wv = f('xa_wv')
    for h in range(2):
        put(('wv', h), _wide(wv[:, h * 512:(h + 1) * 512], 8))
    return out


def _pack_cols(inp):
    cols = np.zeros((128, DEPTH * NCOL), np.float32)

    def vec(v):
        v = np.asarray(v, np.float32)
        return v.reshape(-1, 128).T
    for l in range(DEPTH):
        b = l * NCOL
        for n in ('ffn1_pre', 'ffn1_post', 'mix_pre', 'mix_post', 'xa_pre', 'xa_post', 'ffn2_pre', 'ffn2_post'):
            cols[:, b + COL[n]:b + COL[n] + 8] = vec(inp[n + '_g'][l])
        cols[:, b + COL['mem_g']:b + COL['mem_g'] + 8] = vec(inp['mem_g'][l])
        cols[:, b + COL['b_gate']:b + COL['b_gate'] + 32] = vec(inp['b_gate'][l])
        cols[:, b + COL['conv_a_w']:b + COL['conv_a_w'] + 12] = vec(np.asarray(inp['conv_a_w'][l]).reshape(-1))
        cols[:, b + COL['conv_a_b']:b + COL['conv_a_b'] + 4] = vec(inp['conv_a_b'][l])
        cols[:, b + COL['pool_scale']:b + COL['pool_scale'] + 4] = vec(inp['pool_scale'][l])
        cols[:, b + COL['conv_d_w']:b + COL['conv_d_w'] + 124] = vec(np.asarray(inp['conv_d_w'][l]).reshape(-1))
        cols[:, b + COL['conv_d_b']:b + COL['conv_d_b'] + 4] = vec(inp['conv_d_b'][l])
        cols[:, b + COL['conv_d_ln_g']:b + COL['conv_d_ln_g'] + 4] = vec(inp['conv_d_ln_g'][l])
        cols[:, b + COL['conv_d_ln_b']:b + COL['conv_d_ln_b'] + 4] = vec(inp['conv_d_ln_b'][l])
    return cols


def _prepare(inp):
    x = np.asarray(inp['x'], np.float32)
    mem = np.asarray(inp['mem'], np.float32)
    wstream = np.stack([_pack_layer(inp, l) for l in range(DEPTH)], 0)
    cols = _pack_cols(inp)
    bc = np.empty((DEPTH, 128, 1536), np.float32)
    for l in range(DEPTH):
        row = np.concatenate([np.asarray(inp['sgu_ln_g'][l], np.float32), np.asarray(inp['sgu_ln_b'][l], np.float32),
                              np.asarray(inp['sgu_b'][l], np.float32).reshape(-1)])
        bc[l] = np.broadcast_to(row[None, :], (128, 1536))
    in_maps = []
    for core in range(NCORES):
        b, half = core // 2, core % 2
        xs = np.zeros((TOK, D), np.float32)
        if half == 0:
            xs[HALO:] = x[b, 0:2048]
        else:
            xs[:] = x[b, 2048 - HALO:4096]
        xTc = np.ascontiguousarray(xs.T.reshape(8, 128, TOK).transpose(1, 0, 2))
        memT = np.ascontiguousarray(mem[b].T.reshape(8, 128, NMEM).transpose(1, 0, 2)).reshape(128, 8 * NMEM)
        misc = np.zeros((128, 65 + 128), np.float32)
        misc[:, 65:] = np.eye(128, dtype=np.float32)
        misc[:, 0] = 0.0 if half == 0 else 1.0
        for g in range(4):
            win = 2 << g
            for t in range(16):
                misc[:, 1 + g * 16 + t] = (1.0 / min(t + 1, win)) if half == 0 else (1.0 / win)
        in_maps.append({"xT": xTc, "memT": memT, "wstream": wstream, "cols": cols, "bc": bc, "miscc": misc})
    return in_maps


def _gather(results):
    out = np.empty((4, SEQ, D), np.float32)
    for core in range(NCORES):
        b, half = core // 2, core % 2
        yT = np.asarray(results[core]["yT"])
        y = yT.transpose(1, 0, 2).reshape(D, TOK).T
        out[b, half * 2048:(half + 1) * 2048] = y[HALO:]
    return out


_NC_CACHE = {}


def kernel(**inputs):
    in_maps = _prepare(inputs)
    if 'nc' not in _NC_CACHE:
        _NC_CACHE['nc'] = build_program()
    res = run_bass_kernel_spmd(_NC_CACHE['nc'], in_maps, core_ids=list(range(NCORES)))
    return _gather(res.results)
```
